# Optimizing a Trainium2 kernel written in Bass

```python
import math
import jax, jax.numpy as jnp
from jax import lax
import numpy as np

D_MODEL = 1024
BATCH = 8
SEQ = 2048
DEPTH = 2

RMS_EPS = 1e-5

SSD_EXPAND = 2
SSD_INNER = SSD_EXPAND * D_MODEL
SSD_HEAD_DIM = 64
SSD_HEADS = SSD_INNER // SSD_HEAD_DIM
SSD_STATE = 128
SSD_GROUPS = 4
SSD_HPG = SSD_HEADS // SSD_GROUPS
SSD_CONV = 4
SSD_CHUNK = 128
SSD_CONV_CH = SSD_INNER + 2 * SSD_GROUPS * SSD_STATE

ATTN_HEAD_DIM = 128
ATTN_KV_HEADS = D_MODEL // ATTN_HEAD_DIM
ATTN_PATTERNS = ((128, 1), (512, 4), (2048, 16))
ATTN_N_PAT = len(ATTN_PATTERNS)
ATTN_Q_HEADS = ATTN_N_PAT * ATTN_KV_HEADS
ATTN_OUT = ATTN_KV_HEADS * ATTN_HEAD_DIM
ATTN_BLOCK = 128
ROPE_THETA = 500000.0
ROPE_DIM = ATTN_HEAD_DIM // 4

FFN_HIDDEN = -(-8 * D_MODEL // (3 * 256)) * 256

_SEGMENTS = (SSD_INNER, SSD_CONV_CH, SSD_HEADS, ATTN_Q_HEADS * ATTN_HEAD_DIM,
             ATTN_OUT, ATTN_OUT, D_MODEL, D_MODEL)
IN_SPLITS = tuple(int(s) for s in np.cumsum(_SEGMENTS)[:-1])
N_IN = int(sum(_SEGMENTS))

kernel_name = "hybrid_ssd_dilated_attn_gated_block"


def rms_norm(x, w):
    xf = x.astype(jnp.float32)
    y = xf * lax.rsqrt(jnp.mean(xf * xf, axis=-1, keepdims=True) + RMS_EPS)
    return (y * w.astype(jnp.float32)).astype(x.dtype)


def rope_tables(seq):
    inv = ROPE_THETA ** (-jnp.arange(0, ROPE_DIM, 2, dtype=jnp.float32) / ROPE_DIM)
    ang = jnp.arange(seq, dtype=jnp.float32)[:, None] * inv[None, :]
    return jnp.cos(ang), jnp.sin(ang)


def apply_partial_rope(t, cos, sin):
    half = ROPE_DIM // 2
    shape = (cos.shape[0],) + (1,) * (t.ndim - 3) + (half,)
    c, s = cos.reshape(shape), sin.reshape(shape)
    t1, t2 = t[..., :half], t[..., half:ROPE_DIM]
    return jnp.concatenate([t1 * c - t2 * s, t2 * c + t1 * s, t[..., ROPE_DIM:]], axis=-1).astype(t.dtype)


def causal_depthwise_conv(u, w, b):
    out = lax.conv_general_dilated(
        u, w[:, None, :].astype(u.dtype), window_strides=(1,), padding=((SSD_CONV - 1, 0),),
        dimension_numbers=("NWC", "WIO", "NWC"), feature_group_count=u.shape[-1])
    return out + b.astype(u.dtype)


def ssd_chunked(xh, dt, a, bm, cm):
    bsz, s, _, p = xh.shape
    nc = s // SSD_CHUNK
    x = (xh * dt[..., None]).reshape(bsz, nc, SSD_CHUNK, SSD_GROUPS, SSD_HPG, p)
    a_dt = (dt * a).reshape(bsz, nc, SSD_CHUNK, SSD_GROUPS, SSD_HPG).transpose(0, 3, 4, 1, 2)
    bc = bm.reshape(bsz, nc, SSD_CHUNK, SSD_GROUPS, SSD_STATE)
    cc = cm.reshape(bsz, nc, SSD_CHUNK, SSD_GROUPS, SSD_STATE)
    a_cum = jnp.cumsum(a_dt, axis=-1)
    causal = jnp.tril(jnp.ones((SSD_CHUNK, SSD_CHUNK), dtype=bool))
    decay = jnp.exp(jnp.where(causal, a_cum[..., :, None] - a_cum[..., None, :], -jnp.inf))
    cb = jnp.einsum("bclgn,bcsgn->bgcls", cc, bc)
    y_diag = jnp.einsum("bgecls,bcsgep->bclgep", cb[:, :, None] * decay, x)
    decay_states = jnp.exp(a_cum[..., -1:] - a_cum).transpose(0, 3, 4, 1, 2)
    states = jnp.einsum("bclgn,bclgep->bcgepn", bc, x * decay_states[..., None])
    chunk_decay = jnp.exp(a_cum[..., -1]).transpose(3, 0, 1, 2)

    def step(h, inp):
        s_c, d_c = inp
        return h * d_c[..., None, None] + s_c, h

    h0 = jnp.zeros((bsz, SSD_GROUPS, SSD_HPG, p, SSD_STATE), x.dtype)
    _, states_in = lax.scan(step, h0, (states.transpose(1, 0, 2, 3, 4, 5), chunk_decay))
    state_decay = jnp.exp(a_cum).transpose(0, 3, 4, 1, 2)[..., None]
    y_off = jnp.einsum("bclgn,cbgepn->bclgep", cc, states_in) * state_decay
    return (y_diag + y_off).reshape(bsz, s, SSD_HEADS, p)


def ssd_mixer(z, xbc, dt_raw, conv_w, conv_b, dt_bias, a_log, d_skip, norm_w):
    bsz, s, _ = z.shape
    f32 = jnp.float32
    xbc = jax.nn.silu(causal_depthwise_conv(xbc, conv_w, conv_b)).astype(f32)
    xs, bm, cm = jnp.split(xbc, [SSD_INNER, SSD_INNER + SSD_GROUPS * SSD_STATE], axis=-1)
    xs = xs.reshape(bsz, s, SSD_HEADS, SSD_HEAD_DIM)
    bm = bm.reshape(bsz, s, SSD_GROUPS, SSD_STATE)
    cm = cm.reshape(bsz, s, SSD_GROUPS, SSD_STATE)
    dt = jax.nn.softplus(dt_raw.astype(f32) + dt_bias.astype(f32))
    a = -jnp.exp(a_log.astype(f32))
    y = ssd_chunked(xs, dt, a, bm, cm) + d_skip.astype(f32)[:, None] * xs
    y = y.reshape(bsz, s, SSD_INNER) * jax.nn.silu(z.astype(f32))
    yg = y.reshape(bsz, s, SSD_GROUPS, SSD_INNER // SSD_GROUPS)
    yg = yg * lax.rsqrt(jnp.mean(yg * yg, axis=-1, keepdims=True) + RMS_EPS)
    return (yg.reshape(bsz, s, SSD_INNER) * norm_w.astype(f32)).astype(z.dtype)


def dilated_window_attention(q, k, v, dilation, steps):
    bsz, s, h, dh = q.shape
    length = s // dilation
    nb = -(-length // ATTN_BLOCK)
    lp = nb * ATTN_BLOCK

    def strided(t):
        t = t.reshape(bsz, length, dilation, h, dh).transpose(0, 2, 3, 1, 4)
        return jnp.pad(t, ((0, 0), (0, 0), (0, 0), (0, lp - length), (0, 0)))

    def banded(t):
        tp = jnp.pad(t, ((0, 0), (0, 0), (0, 0), (ATTN_BLOCK, 0), (0, 0)))
        prev = tp[..., :lp, :].reshape(bsz, dilation, h, nb, ATTN_BLOCK, dh)
        cur = t.reshape(bsz, dilation, h, nb, ATTN_BLOCK, dh)
        return jnp.concatenate([prev, cur], axis=-2)

    qb = strided(q).reshape(bsz, dilation, h, nb, ATTN_BLOCK, dh)
    kb, vb = banded(strided(k)), banded(strided(v))
    scores = jnp.einsum("brhnqe,brhnke->brhnqk", qb, kb, preferred_element_type=jnp.float32) * (dh ** -0.5)
    blk = jnp.arange(nb)[:, None, None]
    qi = jnp.arange(ATTN_BLOCK)[None, :, None] + ATTN_BLOCK
    kj = jnp.arange(2 * ATTN_BLOCK)[None, None, :]
    dist = qi - kj
    mask = (dist >= 0) & (dist <= steps) & (blk * ATTN_BLOCK + kj >= ATTN_BLOCK)
    scores = jnp.where(mask, scores, -jnp.inf)
    lse = jax.nn.logsumexp(scores, axis=-1)
    probs = jnp.exp(scores - lse[..., None])
    out = jnp.einsum("brhnqk,brhnke->brhnqe", probs.astype(v.dtype), vb, preferred_element_type=jnp.float32)

    def unstrided(t):
        tail = t.shape[5:]
        t = t.reshape((bsz, dilation, h, lp) + tail)[:, :, :, :length]
        return jnp.moveaxis(t, 3, 1).reshape((bsz, s, h) + tail)

    return unstrided(out), unstrided(lse)


def dilated_attention_mixer(q, k, v, cos, sin):
    bsz, s, _ = q.shape
    q = apply_partial_rope(q.reshape(bsz, s, ATTN_N_PAT, ATTN_KV_HEADS, ATTN_HEAD_DIM), cos, sin)
    k = apply_partial_rope(k.reshape(bsz, s, ATTN_KV_HEADS, ATTN_HEAD_DIM), cos, sin)
    v = v.reshape(bsz, s, ATTN_KV_HEADS, ATTN_HEAD_DIM)
    outs, lses = [], []
    for g, (window, dilation) in enumerate(ATTN_PATTERNS):
        o, l = dilated_window_attention(q[:, :, g], k, v, dilation, window // dilation)
        outs.append(o)
        lses.append(l)
    weights = jax.nn.softmax(jnp.stack(lses), axis=0)
    o = jnp.sum(weights[..., None] * jnp.stack(outs), axis=0)
    return o.reshape(bsz, s, ATTN_OUT).astype(v.dtype)


def setup_inputs(seed: int = 0) -> dict:
    key = jax.random.key(seed)
    ks = jax.random.split(key, 16)
    f32 = jnp.float32

    def dense(k, shape, fan_in):
        return jax.random.normal(k, shape, f32) * fan_in ** -0.5

    def gain(k, shape):
        return 1.0 + 0.02 * jax.random.normal(k, shape, f32)

    dt0 = jnp.exp(jax.random.uniform(ks[5], (DEPTH, SSD_HEADS), f32, math.log(1e-3), math.log(1e-1)))
    return {
        "x": jax.random.normal(ks[0], (BATCH, SEQ, D_MODEL), f32),
        "norm_mix": gain(ks[1], (DEPTH, D_MODEL)),
        "w_in": dense(ks[2], (DEPTH, D_MODEL, N_IN), D_MODEL),
        "conv_w": dense(ks[3], (DEPTH, SSD_CONV, SSD_CONV_CH), SSD_CONV),
        "conv_b": 0.02 * jax.random.normal(ks[4], (DEPTH, SSD_CONV_CH), f32),
        "dt_bias": dt0 + jnp.log(-jnp.expm1(-dt0)),
        "a_log": jnp.log(jax.random.uniform(ks[6], (DEPTH, SSD_HEADS), f32, 1.0, 16.0)),
        "d_skip": gain(ks[7], (DEPTH, SSD_HEADS)),
        "ssd_norm": gain(ks[8], (DEPTH, SSD_INNER)),
        "w_ssd_branch": dense(ks[9], (DEPTH, SSD_INNER, D_MODEL), SSD_INNER),
        "w_attn_branch": dense(ks[10], (DEPTH, ATTN_OUT, D_MODEL), ATTN_OUT),
        "w_out": dense(ks[11], (DEPTH, D_MODEL, D_MODEL), D_MODEL),
        "norm_ffn": gain(ks[12], (DEPTH, D_MODEL)),
        "w_gate_up": dense(ks[13], (DEPTH, D_MODEL, 2 * FFN_HIDDEN), D_MODEL),
        "w_down": dense(ks[14], (DEPTH, FFN_HIDDEN, D_MODEL), FFN_HIDDEN),
        "norm_final": gain(ks[15], (D_MODEL,)),
    }


def reference(x, norm_mix, w_in, conv_w, conv_b, dt_bias, a_log, d_skip, ssd_norm,
              w_ssd_branch, w_attn_branch, w_out, norm_ffn, w_gate_up, w_down, norm_final):
    cos, sin = rope_tables(x.shape[1])
    h = x
    for layer in range(DEPTH):
        u = rms_norm(h, norm_mix[layer])
        proj = u @ w_in[layer]
        z, xbc, dt_raw, q, k, v, g_ssd, g_attn = jnp.split(proj, IN_SPLITS, axis=-1)
        y_ssd = ssd_mixer(z, xbc, dt_raw, conv_w[layer], conv_b[layer], dt_bias[layer],
                          a_log[layer], d_skip[layer], ssd_norm[layer])
        y_attn = dilated_attention_mixer(q, k, v, cos, sin)
        merged = (jax.nn.sigmoid(g_ssd) * (y_ssd @ w_ssd_branch[layer])
                  + jax.nn.sigmoid(g_attn) * (y_attn @ w_attn_branch[layer]))
        h = h + merged @ w_out[layer]
        u = rms_norm(h, norm_ffn[layer])
        gate, up = jnp.split(u @ w_gate_up[layer], 2, axis=-1)
        h = h + (jax.nn.silu(gate) * up) @ w_down[layer]
    return rms_norm(h, norm_final)
```

```python
import types
import numpy as np
import concourse.bass as bass
import concourse.mybir as mybir
from concourse.bass_utils import run_bass_kernel_spmd

F32 = mybir.dt.float32
BF16 = mybir.dt.bfloat16
AF = mybir.ActivationFunctionType
ALU = mybir.AluOpType

S = 2048
D = 1024
DEPTH = 2
NT = 16
TG = 4
N_IN = 12320
SSD_INNER = 2048
FFN = 2816
NJ = FFN // 128
EPS = 1e-5
R_Z = 0
R_XBC = 2048
R_Q = 5120
R_K = 8192
R_V = 9216
R_GS = 10240
R_GA = 11264
NPROJ = 12288
C_DT = 5120


def col_of_row(r):
    return r if r < 5120 else r + 32


import os
DBG = os.environ.get('KDBG', '').split(',')
NDSEM = 40
PIPE_SSD = True
SBUF_BASE = 16640
SBUF_END = 229376 - 128


def _freeze(fn):
    if fn.__closure__ is None:
        return fn
    cells = []
    for c in fn.__closure__:
        try:
            cells.append(types.CellType(c.cell_contents))
        except ValueError:
            cells.append(c)
    return types.FunctionType(fn.__code__, fn.__globals__, fn.__name__, fn.__defaults__, tuple(cells))


class Op:
    __slots__ = ("eng", "fn", "deps", "need_inc", "is_dma", "dslot", "dval", "done")

    def __init__(self, eng, fn, is_dma):
        self.eng = eng
        self.fn = fn
        self.deps = ()
        self.need_inc = False
        self.is_dma = is_dma
        self.dslot = -1
        self.dval = 0
        self.done = None


class Sched:
    ENGS = ("pe", "act", "dve", "pool", "sp")

    def __init__(self, nc):
        self.nc = nc
        self.ops = {e: [] for e in self.ENGS}
        self.lastw = {}
        self.readers = {}
        self.ndma = 0
        self.sb_off = SBUF_BASE
        self.sb_stack = []
        self.nalloc = 0
        self.dma_since = []
        self.last_compute = {}

    def push(self):
        self.sb_stack.append(self.sb_off)

    def pop(self):
        self.sb_off = self.sb_stack.pop()
        self.barrier()

    def barrier(self):
        targets = set(self.dma_since)
        for e, i in self.last_compute.items():
            targets.add((e, i))
        for eng in self.ENGS:
            op = Op(eng, lambda e: e.nop(), False)
            idx = len(self.ops[eng])
            deps = {d for d in targets if not (d[0] == eng and not self.ops[d[0]][d[1]].is_dma)}
            op.deps = deps
            for d in deps:
                self.ops[d[0]][d[1]].need_inc = True
            self.ops[eng].append(op)
        self.dma_since = []

    def tile(self, shape, dtype, name=None):
        esz = 4 if dtype == F32 else 2
        n = 1
        for d in shape[1:]:
            n *= d
        nbytes = (n * esz + 63) // 64 * 64
        off = self.sb_off
        assert off + nbytes <= SBUF_END, f"SBUF overflow {name} {off + nbytes}"
        self.sb_off = off + nbytes
        self.nalloc += 1
        nm = f"{name or 't'}_{self.nalloc}"
        return self.nc.alloc_sbuf_tensor_at(nm, list(shape), dtype, offset=off)

    def add(self, eng, fn, reads=(), writes=(), dma=False):
        op = Op(eng, _freeze(fn), dma)
        idx = len(self.ops[eng])
        self.ops[eng].append(op)
        me = (eng, idx)
        deps = set()
        psr = [k for k in reads if isinstance(k, tuple) and k[0] == "ps"]
        if psr:
            reads = [k for k in reads if not (isinstance(k, tuple) and k[0] == "ps")]
            writes = list(writes) + [k for k in psr if k not in writes]
        for k in reads:
            w = self.lastw.get(k)
            if w is not None:
                deps.add(w)
        for k in writes:
            w = self.lastw.get(k)
            if w is not None:
                deps.add(w)
            for r in self.readers.get(k, ()):
                deps.add(r)
        deps.discard(me)
        if eng == "pe":
            deps = {d for d in deps if not (d[0] == "pe")}
        op.deps = deps
        for d in deps:
            self.ops[d[0]][d[1]].need_inc = True
        for k in reads:
            self.readers.setdefault(k, []).append(me)
        for k in writes:
            self.lastw[k] = me
            self.readers[k] = []
        if dma:
            op.dslot = self.ndma % NDSEM
            op.dval = 16 * (self.ndma // NDSEM + 1)
            self.ndma += 1
            self.dma_since.append(me)
        else:
            self.last_compute[eng] = idx
        return op

    def dma(self, eng, out, in_, reads=(), writes=()):
        return self.add(eng, lambda e: e.dma_start(out=out, in_=in_), reads, writes, dma=True)

    def mm(self, out, lhsT, rhs, start, stop, reads=(), writes=()):
        return self.add("pe", lambda e: e.matmul(out, lhsT=lhsT, rhs=rhs, start=start, stop=stop), reads, writes)

    def tr(self, out, in_, ident, reads=(), writes=()):
        return self.add("pe", lambda e: e.transpose(out, in_, ident), reads, writes)

    def emit(self, sems, dsems):
        for eng in self.ENGS:
            cnt = 0
            for op in self.ops[eng]:
                if op.is_dma:
                    op.done = (("d", op.dslot), op.dval)
                elif op.need_inc:
                    cnt += 1
                    op.done = ((eng,), cnt)
        semof = {}
        for e in self.ENGS:
            semof[(e,)] = sems[e]
        for i in range(NDSEM):
            semof[("d", i)] = dsems[i]
        nc = self.nc
        stats = {}

        def run(eng, e):
            waited = {}
            nw = 0
            for op in self.ops[eng]:
                want = {}
                for d in op.deps:
                    sk, v = self.ops[d[0]][d[1]].done
                    if want.get(sk, 0) < v:
                        want[sk] = v
                if op.is_dma and op.dval > 16:
                    sk = ("d", op.dslot)
                    if want.get(sk, 0) < op.dval - 16:
                        want[sk] = op.dval - 16
                for sk, v in want.items():
                    if waited.get(sk, 0) < v:
                        e.wait_ge(semof[sk], v)
                        waited[sk] = v
                        nw += 1
                ins = op.fn(e)
                if op.is_dma:
                    ins.then_inc(semof[("d", op.dslot)], 16)
                elif op.need_inc:
                    ins.then_inc(semof[(eng,)], 1)
            stats[eng] = (len(self.ops[eng]), nw)

        with nc.Block() as block:
            @block.tensor
            def _(e):
                run("pe", e)

            @block.scalar
            def _(e):
                run("act", e)

            @block.vector
            def _(e):
                run("dve", e)

            @block.gpsimd
            def _(e):
                run("pool", e)

            @block.sync
            def _(e):
                run("sp", e)
        return stats


def build(nc, n_layers=DEPTH, dbg=False, stop_after=None):
    def din(name, shape, dt=F32):
        return nc.dram_tensor(name, list(shape), dt, kind="ExternalInput").ap()

    x = din("x", [S, D])
    w_in = din("w_in", [DEPTH, D, N_IN])
    w_a = din("w_ssd_branch", [DEPTH, SSD_INNER, D])
    w_b = din("w_attn_branch", [DEPTH, D, D])
    w_o = din("w_out", [DEPTH, D, D])
    w_gu = din("w_gate_up", [DEPTH, D, 2 * FFN])
    w_d = din("w_down", [DEPTH, FFN, D])
    nmix = din("nmix", [DEPTH, 128, 8])
    nffn = din("nffn", [DEPTH, 128, 8])
    nfin = din("nfin", [128, D])
    convw = din("convw", [DEPTH, 128, 24 * 4])
    convb = din("convb", [DEPTH, 128, 24])
    dtb = din("dtb", [DEPTH, 128, 32])
    alog = din("alog", [DEPTH, 128, 32])
    dskip = din("dskip", [DEPTH, 128, 16])
    ssdn = din("ssdn", [DEPTH, 128, 16])
    c_amask = din("c_amask", [128, 2944])
    c_tri = din("c_tri", [128, 128])
    c_ident = din("c_ident", [128, 128])
    c_rope = din("c_rope", [32, 2 * S])
    c_rsw = din("c_rsw", [32, 32])
    out = nc.dram_tensor("out", [S, D], F32, kind="ExternalOutput").ap()

    def dscr(name, shape, dt):
        if dbg:
            return nc.dram_tensor(name, list(shape), dt, kind="ExternalOutput").ap()
        return nc.dram_tensor(name, list(shape), dt, kind="Internal").ap()

    hD = dscr("h_scr", [S, D], F32)
    projT = dscr("projT", [NPROJ, S], BF16)
    yssdT = dscr("yssdT", [SSD_INNER, S], BF16)
    yattnT = dscr("yattnT", [D, S], BF16)

    sc = Sched(nc)
    banks = [nc.alloc_psum_tensor(f"bank{i}", [128, 512], F32) for i in range(8)]

    def bk(i):
        return banks[i]

    def bkb(i):
        return banks[i][:, :].bitcast(BF16)

    PK = [("ps", i) for i in range(8)]

    ident_f = sc.tile([128, 128], F32, "identf")
    ident_b = sc.tile([128, 128], BF16, "identb")
    ones_f = sc.tile([128, 128], F32, "onesf")
    tri_f = sc.tile([128, 128], F32, "trif")
    amask = sc.tile([128, 2944], BF16, "amask")
    ropeC = sc.tile([32, S], F32, "ropeC")
    ropeS = sc.tile([32, S], F32, "ropeS")
    rsw = sc.tile([32, 32], BF16, "rsw")
    dtraw = sc.tile([128, 512], F32, "dtraw")
    epsb = sc.tile([128, 1], F32, "epsb")
    sc.dma("sp", ident_f[:], c_ident[:, :], writes=["identf"])
    sc.dma("pool", ident_b[:], c_ident[:, :], writes=["identb"])
    sc.dma("sp", tri_f[:], c_tri[:, :], writes=["trif"])
    sc.dma("pool", amask[:], c_amask[:, :], writes=["amask"])
    sc.dma("sp", ropeC[:], c_rope[:, 0:S], writes=["ropeC"])
    sc.dma("sp", ropeS[:], c_rope[:, S:2 * S], writes=["ropeS"])
    sc.dma("pool", rsw[:], c_rsw[:, :], writes=["rsw"])
    tri_b = sc.tile([128, 128], BF16, "trib")
    sc.dma("pool", tri_b[:], c_tri[:, :], writes=["trib"])
    sc.add("pool", lambda e: e.memset(ones_f[:], 1.0), writes=["onesf"])
    ones_b = sc.tile([128, 128], BF16, "onesb")
    sc.add("pool", lambda e: e.memset(ones_b[:], 1.0), writes=["onesb"])
    sc.add("pool", lambda e: e.memset(epsb[:], EPS), writes=["epsb"])

    HK = [("h", tb) for tb in range(NT)]

    def rms_stats(ht, htk, ss_col, rstd_col, tmpk, sqjunk, sqk, n_feat):
        sc.add("act", lambda e: e.activation(out=sqjunk, in_=ht, func=AF.Square, accum_out=ss_col),
               reads=[htk], writes=[sqk, tmpk])
        sc.add("act", lambda e: e.activation(out=rstd_col, in_=ss_col, func=AF.Sqrt, bias=epsb[:, 0:1], scale=1.0 / n_feat),
               reads=[tmpk, "epsb"], writes=[tmpk])
        sc.add("dve", lambda e: e.reciprocal(out=rstd_col, in_=rstd_col), reads=[tmpk], writes=[tmpk])

    def norm_to_uT(uT, nw_dram, tagp, src=None):
        srcD = hD if src is None else src
        sc.push()
        nw = sc.tile([128, 8], F32, "nw")
        sc.dma("sp", nw[:], nw_dram, writes=[tagp + "nw"])
        hall = sc.tile([128, NT, D], F32, "hall")
        ubs = [sc.tile([128, D], BF16, "ub") for _ in range(3)]
        junk = sc.tile([128, D], BF16, "junk")
        st = sc.tile([128, 2 * NT], F32, "nstat")
        for tb in range(NT):
            rk = [("h", tb)] if src is None else []
            sc.dma("sp", hall[:, tb, :], srcD[tb * 128:(tb + 1) * 128, :], reads=rk, writes=[(tagp + "hall", tb)])
            sc.add("act", lambda e, tb=tb: e.activation(out=junk[:], in_=hall[:, tb, :], func=AF.Square,
                                                      accum_out=st[:, tb:tb + 1]),
                   reads=[(tagp + "hall", tb)], writes=[tagp + "junk", (tagp + "ss", tb)])
        SSK = [(tagp + "ss", tb) for tb in range(NT)]
        sc.add("act", lambda e: e.activation(out=st[:, NT:2 * NT], in_=st[:, 0:NT], func=AF.Ln, bias=epsb[:, 0:1],
                                           scale=1.0 / D), reads=SSK + ["epsb"], writes=[tagp + "rstd"])
        sc.add("act", lambda e: e.activation(out=st[:, NT:2 * NT], in_=st[:, NT:2 * NT], func=AF.Exp, scale=-0.5),
               reads=[tagp + "rstd"], writes=[tagp + "rstd"])
        for tb in range(NT):
            b = tb % 3
            ub = ubs[b]
            ubk = (tagp + "ub", b)
            sc.add("act", lambda e, ub=ub, tb=tb: e.activation(out=ub[:], in_=hall[:, tb, :], func=AF.Identity,
                                                             scale=st[:, NT + tb:NT + tb + 1]),
                   reads=[(tagp + "hall", tb), tagp + "rstd"], writes=[ubk])
            pb = tb % 2
            for c in range(8):
                sc.tr(bkb(pb)[:, c * 128:(c + 1) * 128], ub[:, c * 128:(c + 1) * 128], ident_b[:],
                      reads=[ubk, "identb"], writes=[PK[pb]])
            sc.add("dve", lambda e, pb=pb, tb=tb: e.tensor_tensor(
                out=uT[:, :, tb * 128:(tb + 1) * 128],
                in0=bkb(pb)[:, :].rearrange("p (c t) -> p c t", c=8),
                in1=nw[:, :].unsqueeze(2).broadcast_to([128, 8, 128]), op=ALU.mult),
                reads=[PK[pb], tagp + "nw"], writes=[("uT", tb // 4)])
        sc.pop()

    def load_w(eng, dst, src, key):
        sc.dma(eng, dst, src, writes=[key])

    for L in range(n_layers):
        sc.push()
        uT = sc.tile([128, 8, S], BF16, "uT")
        norm_to_uT(uT, nmix[L], f"n{L}a", src=(x if L == 0 else None))
        if L == 0:
            sc.dma("sp", hD[:, :], x[:, :], writes=HK)
        UK = [("uT", g) for g in range(TG)]

        sc.push()
        wts = [sc.tile([128, 8, 128], BF16, "wt") for _ in range(3)]
        stg = [sc.tile([128, S], BF16, "stg") for _ in range(2)]
        w_in_v = w_in[L].rearrange("(c p) n -> p c n", p=128)
        for ch in range(NPROJ // 128):
            r0 = ch * 128
            c0 = col_of_row(r0)
            wt = wts[ch % 3]
            wk = ("ipw", ch % 3)
            load_w("pool", wt[:], w_in_v[:, :, c0:c0 + 128], wk)
            sg = stg[ch % 2]
            sk = ("ipstg", ch % 2)
            if r0 < R_XBC:
                fn = AF.Silu
            elif r0 >= R_GS:
                fn = AF.Sigmoid
            else:
                fn = None
            for tg in range(TG):
                pb = (ch * TG + tg) % 4
                for c in range(8):
                    sc.mm(bk(pb)[:, :], wt[:, c, :], uT[:, c, tg * 512:(tg + 1) * 512], c == 0, c == 7,
                          reads=[wk, UK[tg]], writes=[PK[pb]])
                dst = sg[:, tg * 512:(tg + 1) * 512]
                if fn is not None:
                    sc.add("act", lambda e, dst=dst, pb=pb, fn=fn: e.activation(out=dst, in_=bk(pb)[:, :], func=fn),
                           reads=[PK[pb]], writes=[sk])
                elif tg % 2 == 0:
                    sc.add("act", lambda e, dst=dst, pb=pb: e.copy(out=dst, in_=bk(pb)[:, :]),
                           reads=[PK[pb]], writes=[sk])
                else:
                    sc.add("dve", lambda e, dst=dst, pb=pb: e.tensor_copy(out=dst, in_=bk(pb)[:, :]),
                           reads=[PK[pb]], writes=[sk])
            sc.dma("sp", projT[r0:r0 + 128, :], sg[:], reads=[sk], writes=[("projT", ch)])
        wdt = sc.tile([128, 8, 32], BF16, "wdt")
        load_w("pool", wdt[:], w_in_v[:, :, C_DT:C_DT + 32], "wdt")
        for tb in range(NT):
            for c in range(8):
                sc.mm(bk(4)[:, tb * 32:(tb + 1) * 32], uT[:, c, tb * 128:(tb + 1) * 128], wdt[:, c, :], c == 0, c == 7,
                      reads=["wdt", UK[tb // 4]], writes=[PK[4]])
        sc.add("dve", lambda e: e.tensor_copy(out=dtraw[:], in_=bk(4)[:, :]), reads=[PK[4]], writes=["dtraw"])
        sc.pop()
        sc.pop()
        if stop_after == f"inproj{L}":
            break

        def PJ(r0, n=128):
            return [("projT", r // 128) for r in range(r0, r0 + n, 128)]

        sc.push()
        NB = 2
        qk = [[sc.tile([128, S], BF16, "qk") for _ in range(4)] for _ in range(NB)]
        vT = [sc.tile([128, S], BF16, "vT") for _ in range(NB)]
        V1 = [sc.tile([128, NT, 129], BF16, "V1") for _ in range(NB)]
        t1 = sc.tile([32, S], F32, "ropet1")
        t2s = [sc.tile([32, 512], F32, "ropet2") for _ in range(2)]
        NPT = 10
        SBANKS = (2, 3, 4, 5, 7)
        NSB = len(SBANKS)
        LA = 4
        PTs = [sc.tile([128, 512], BF16, "PT") for _ in range(NPT)]
        ytok = [sc.tile([128, 512], BF16, "ytok") for _ in range(2)]
        ystg = [sc.tile([128, S], BF16, "aystg") for _ in range(2)]
        rden = sc.tile([128, NT], F32, "rden")
        for b in range(NB):
            sc.add("pool", lambda e, b=b: e.memset(V1[b][:], 1.0), writes=[("V1", b)])
        scale = float(128 ** -0.5)
        GOFF = (0, 256, 896)
        GLEN = (2, 5, 16)

        def attn_prologue(hd):
            b = hd % NB
            QK = [("qk", b, i) for i in range(4)]
            for g in range(3):
                r0 = R_Q + g * 1024 + hd * 128
                sc.dma("sp", qk[b][g][:], projT[r0:r0 + 128, :], reads=PJ(r0), writes=[QK[g]])
            r0 = R_K + hd * 128
            sc.dma("sp", qk[b][3][:], projT[r0:r0 + 128, :], reads=PJ(r0), writes=[QK[3]])
            r0 = R_V + hd * 128
            sc.dma("sp", vT[b][:], projT[r0:r0 + 128, :], reads=PJ(r0), writes=[("vT", b)])
            yield
            for i in range(4):
                qt = qk[b][i]
                sc.add("dve", lambda e, qt=qt: e.tensor_tensor(out=t1[:], in0=qt[0:32, :], in1=ropeC[:], op=ALU.mult),
                       reads=[QK[i], "ropeC"], writes=["ropet1"])
                yield
                for tg in range(TG):
                    sl = slice(tg * 512, (tg + 1) * 512)
                    pb = 6
                    t2 = t2s[tg % 2]
                    t2k = ("ropet2", tg % 2)
                    sc.mm(bk(pb)[0:32, :], rsw[:, :], qt[0:32, sl], True, True, reads=["rsw", QK[i]], writes=[PK[pb]])
                    sc.add("dve", lambda e, pb=pb, sl=sl, t2=t2: e.tensor_tensor(out=t2[:], in0=bk(pb)[0:32, :],
                                                                               in1=ropeS[:, sl], op=ALU.mult),
                           reads=[PK[pb], "ropeS"], writes=[t2k])
                    sc.add("dve", lambda e, qt=qt, sl=sl, t2=t2: e.tensor_tensor(out=qt[0:32, sl], in0=t1[:, sl], in1=t2[:],
                                                                               op=ALU.add),
                           reads=["ropet1", t2k], writes=[QK[i]])
                    yield
            for half in range(2):
                pb = 6
                for j in range(8):
                    tb = half * 8 + j
                    sc.tr(bkb(pb)[:, j * 128:(j + 1) * 128], vT[b][:, tb * 128:(tb + 1) * 128], ident_b[:],
                          reads=[("vT", b), "identb"], writes=[PK[pb]])
                sc.add("act", lambda e, pb=pb, half=half, b=b: e.copy(
                    out=V1[b][:, half * 8:(half + 1) * 8, 0:128],
                    in_=bkb(pb)[:, :].rearrange("p (j d) -> p j d", j=8)),
                    reads=[PK[pb]], writes=[("V1", b)])
                yield

        def attn_core(hd, gen):
            b = hd % NB
            QK = [("qk", b, i) for i in range(4)]
            kT = qk[b][3]
            runs_all = []
            for qb in range(NT):
                pairs = []
                for g, nprev in enumerate((1, 4, qb)):
                    lo = max(0, qb - nprev)
                    for kb in range(lo, qb + 1):
                        mi = GLEN[g] - 1 - (qb - kb)
                        pairs.append((g, kb, GOFF[g] + mi * 128))
                runs = []
                for p in pairs:
                    if runs and runs[-1][0][0] == p[0] and len(runs[-1]) < 4:
                        runs[-1].append(p)
                    else:
                        runs.append([p])
                for ri, run in enumerate(runs):
                    runs_all.append((qb, run, ri == 0, ri == len(runs) - 1))
            nr = len(runs_all)
            for i in range(nr + LA):
                if i % 3 == 2:
                    next(gen, None)
                if i < nr:
                    qb, run, isf, isl = runs_all[i]
                    qsl = slice(qb * 128, (qb + 1) * 128)
                    sb_ = SBANKS[i % NSB]
                    PT = PTs[i % NPT]
                    ptk = ("PT", i % NPT)
                    n = len(run)
                    for j, (g, kb, mc) in enumerate(run):
                        sc.mm(bk(sb_)[:, j * 128:(j + 1) * 128], kT[:, kb * 128:(kb + 1) * 128], qk[b][g][:, qsl], True, True,
                              reads=[QK[3], QK[g]], writes=[PK[sb_]])
                    sc.add("act", lambda e, PT=PT, sb_=sb_, n=n: e.activation(out=PT[:, 0:n * 128], in_=bk(sb_)[:, 0:n * 128],
                                                                            func=AF.Exp, scale=scale),
                           reads=[PK[sb_]], writes=[ptk])
                    mc0 = run[0][2]
                    sc.add("dve",
                           lambda e, PT=PT, n=n, mc0=mc0: e.tensor_tensor(out=PT[:, 0:n * 128], in0=PT[:, 0:n * 128],
                                                                          in1=amask[:, mc0:mc0 + n * 128], op=ALU.mult),
                           reads=[ptk, "amask"], writes=[ptk])
                jx = i - LA
                if jx < 0:
                    continue
                qb, run, isf, isl = runs_all[jx]
                ob = qb % 2
                PT = PTs[jx % NPT]
                ptk = ("PT", jx % NPT)
                n = len(run)
                for j, (g, kb, mc) in enumerate(run):
                    sc.mm(bk(ob)[:, 0:129], PT[:, j * 128:(j + 1) * 128], V1[b][:, kb, :], isf and j == 0, isl and j == n - 1,
                          reads=[ptk, ("V1", b)], writes=[PK[ob]])
                if not isl:
                    continue
                yb = (qb // 4) % 2
                ytk = ("ytok", yb)
                sc.add("dve", lambda e, ob=ob, qb=qb: e.reciprocal(out=rden[:, qb:qb + 1], in_=bk(ob)[:, 128:129]),
                       reads=[PK[ob]], writes=[("rden", qb)])
                sc.add("act", lambda e, ob=ob, qb=qb, yb=yb: e.activation(
                    out=ytok[yb][:, (qb % 4) * 128:(qb % 4 + 1) * 128], in_=bk(ob)[:, 0:128], func=AF.Identity,
                    scale=rden[:, qb:qb + 1]), reads=[PK[ob], ("rden", qb)], writes=[ytk])
                if qb % 4 == 3:
                    pb = 6
                    for j in range(4):
                        sc.tr(bkb(pb)[:, j * 128:(j + 1) * 128], ytok[yb][:, j * 128:(j + 1) * 128], ident_b[:],
                              reads=[ytk, "identb"], writes=[PK[pb]])
                    tg = qb // 4
                    sc.add("dve", lambda e, pb=pb, tg=tg, hd=hd: e.tensor_copy(
                        out=ystg[hd % 2][:, tg * 512:(tg + 1) * 512], in_=bkb(pb)[:, 0:512]),
                        reads=[PK[pb]], writes=[("aystg", hd % 2)])
            for _ in gen:
                pass
            sc.dma("sp", yattnT[hd * 128:(hd + 1) * 128, :], ystg[hd % 2][:], reads=[("aystg", hd % 2)],
                   writes=[("yattnT", hd)])

        for _ in attn_prologue(0):
            pass
        for hd in range(8):
            attn_core(hd, attn_prologue(hd + 1) if hd + 1 < 8 else iter(()))
        sc.pop()
        if stop_after == f"attn{L}":
            break


        sc.push()
        sm = {}
        for nm in ("dtb", "alog"):
            sm[nm] = sc.tile([128, 32], F32, nm)
        sc.dma("sp", sm["dtb"][:], dtb[L], writes=["dtb"])
        sc.dma("sp", sm["alog"][:], alog[L], writes=["alog"])
        cw = sc.tile([128, 96], F32, "cw")
        cb = sc.tile([128, 24], F32, "cb")
        dsk = sc.tile([128, 16], F32, "dsk")
        snw = sc.tile([128, 16], F32, "snw")
        sc.dma("sp", cw[:], convw[L], writes=["cw"])
        sc.dma("sp", cb[:], convb[L], writes=["cb"])
        sc.dma("sp", dsk[:], dskip[L], writes=["dsk"])
        sc.dma("sp", snw[:], ssdn[L], writes=["snw"])
        dt_t = sc.tile([128, 512], F32, "dt_t")
        adt = sc.tile([128, 512], F32, "adt")
        ack = sc.tile([128, 512], F32, "ack")
        dtw = sc.tile([128, 512], F32, "dtw")
        cdec = sc.tile([128, 512], F32, "cdec")
        tmpA = sc.tile([128, 512], F32, "tmpA")
        apos = sc.tile([128, 32], F32, "apos")

        def v3(t):
            return t[:, :].rearrange("p (c h) -> p c h", h=32)

        def b3(t):
            return t[:, :].unsqueeze(1).broadcast_to([128, 16, 32])

        sc.add("dve", lambda e: e.tensor_tensor(out=v3(tmpA), in0=v3(dtraw), in1=b3(sm["dtb"]), op=ALU.add),
               reads=["dtraw", "dtb"], writes=["tmpA"])
        sc.add("act", lambda e: e.activation(out=tmpA[:], in_=tmpA[:], func=AF.Exp), reads=["tmpA"], writes=["tmpA"])
        sc.add("act", lambda e: e.activation(out=dt_t[:], in_=tmpA[:], func=AF.Ln, bias=1.0, scale=1.0),
               reads=["tmpA"], writes=["dt_t"])
        sc.add("act", lambda e: e.activation(out=apos[:], in_=sm["alog"][:], func=AF.Exp), reads=["alog"], writes=["apos"])
        sc.add("dve", lambda e: e.scalar_tensor_tensor(out=v3(adt), in0=v3(dt_t), scalar=-1.0, in1=b3(apos),
                                                     op0=ALU.mult, op1=ALU.mult),
               reads=["dt_t", "apos"], writes=["adt"])
        adt3 = [sc.tile([128, 512], BF16, "adt3") for _ in range(3)]
        rres = sc.tile([128, 512], F32, "rres")
        sc.add("dve", lambda e: e.tensor_copy(out=adt3[0][:], in_=adt[:]), reads=["adt"], writes=[("adt3", 0)])
        sc.add("dve", lambda e: e.tensor_tensor(out=rres[:], in0=adt[:], in1=adt3[0][:], op=ALU.subtract),
               reads=["adt", ("adt3", 0)], writes=["rres"])
        sc.add("dve", lambda e: e.tensor_copy(out=adt3[1][:], in_=rres[:]), reads=["rres"], writes=[("adt3", 1)])
        sc.add("dve", lambda e: e.tensor_tensor(out=rres[:], in0=rres[:], in1=adt3[1][:], op=ALU.subtract),
               reads=["rres", ("adt3", 1)], writes=["rres"])
        sc.add("dve", lambda e: e.tensor_copy(out=adt3[2][:], in_=rres[:]), reads=["rres"], writes=[("adt3", 2)])
        sc.mm(bk(7)[:, :], tri_f[:], adt[:], True, True, reads=["trif", "adt"], writes=[PK[7]])
        sc.mm(bk(1)[:, :], ones_f[:], adt[:], True, True, reads=["onesf", "adt"], writes=[PK[1]])
        sc.add("dve", lambda e: e.tensor_copy(out=ack[:], in_=bk(7)[:, :]), reads=[PK[7]], writes=["ack"])
        nack = sc.tile([128, 512], F32, "nack")
        sc.add("dve", lambda e: e.tensor_scalar(out=nack[:], in0=ack[:], scalar1=-1.0, scalar2=None, op0=ALU.mult),
               reads=["ack"], writes=["nack"])
        sc.add("dve", lambda e: e.tensor_tensor(out=tmpA[:], in0=bk(1)[:, :], in1=ack[:], op=ALU.subtract),
               reads=[PK[1], "ack", "tmpA"], writes=["tmpA"])
        sc.add("act", lambda e: e.activation(out=tmpA[:], in_=tmpA[:], func=AF.Exp), reads=["tmpA"], writes=["tmpA"])
        sc.add("dve", lambda e: e.tensor_tensor(out=dtw[:], in0=dt_t[:], in1=tmpA[:], op=ALU.mult),
               reads=["tmpA", "dt_t"], writes=["dtw"])
        sc.add("act", lambda e: e.activation(out=cdec[:], in_=bk(1)[:, :], func=AF.Exp), reads=[PK[1]], writes=["cdec"])

        pre = [sc.tile([128, 3 + S], BF16, "pre") for _ in range(2)]
        for i in range(2):
            sc.add("pool", lambda e, i=i: e.memset(pre[i][:, 0:3], 0.0), writes=[("pre", i)])
        xc = [sc.tile([128, S], BF16, "xc") for _ in range(6)]
        diag = [sc.tile([128, 128], BF16, "diag") for _ in range(8)]
        sz = [sc.tile([128, 4, 512], BF16, "sz") for _ in range(2)]
        xdt = [sc.tile([128, 512], BF16, "xdt") for _ in range(2)]
        xdw = [sc.tile([128, 512], BF16, "xdw") for _ in range(2)]
        btok = [sc.tile([128, 128], BF16, "btok") for _ in range(2)]
        cbm = [sc.tile([128, 128], F32, "cbm") for _ in range(2)]
        HT = sc.tile([128, 512], F32, "HT")
        HTb = [sc.tile([128, 512], BF16, "HTb") for _ in range(2)]
        argt = [sc.tile([128, 128], F32, "arg") for _ in range(16)]
        dect = [sc.tile([128, 128], F32, "dec") for _ in range(16)]
        Mh = [sc.tile([128, 128], BF16, "Mh") for _ in range(16)]
        Et = [sc.tile([128, 128], F32, "E") for _ in range(16)]
        Ct = [sc.tile([128, 128], BF16, "Ct") for _ in range(16)]
        yv = [sc.tile([128, 128], F32, "yv") for _ in range(4)]
        yg = [sc.tile([128, 128], F32, "yg") for _ in range(8)]
        sq = [sc.tile([128, 128], BF16, "sq") for _ in range(4)]
        rbc = sc.tile([128, 128], F32, "rbc")
        yst = [sc.tile([128, 4, 512], BF16, "yst") for _ in range(2)]
        yssd_v = yssdT.rearrange("(k p) t -> p k t", p=128)
        projT_v = projT.rearrange("(k p) t -> p k t", p=128)
        YK = [PK[5]] * 4
        BANKK = {6: [PK[6]], 7: [PK[7]]}
        nd = 0
        for gi in range(4):
            chunks = [4 * gi + i for i in range(4)] + [16 + gi, 20 + gi]
            for j, chx in enumerate(chunks):
                dg = []
                for k in range(4):
                    di = nd % 8
                    nd += 1
                    col = chx * 4 + k
                    sc.add("act", lambda e, di=di, col=col: e.activation(
                        out=diag[di][:], in_=ident_f[:], func=AF.Identity, scale=cw[:, col:col + 1]),
                        reads=["identf", "cw"], writes=[("diag", di)])
                    dg.append(di)
                pi = (gi * 6 + j) % 2
                r0 = R_XBC + chx * 128
                sc.dma("sp", pre[pi][:, 3:3 + S], projT[r0:r0 + 128, :], reads=PJ(r0), writes=[("pre", pi)])
                for tg in range(TG):
                    pb = 6 + (tg % 2)
                    for k in range(4):
                        sc.mm(bk(pb)[:, :], diag[dg[k]][:], pre[pi][:, tg * 512 + k:tg * 512 + k + 512], k == 0, k == 3,
                              reads=[("diag", dg[k]), ("pre", pi)], writes=BANKK[pb])
                    sc.add("act", lambda e, j=j, tg=tg, pb=pb, chx=chx: e.activation(
                        out=xc[j][:, tg * 512:(tg + 1) * 512], in_=bk(pb)[:, :], func=AF.Silu, bias=cb[:, chx:chx + 1]),
                        reads=BANKK[pb] + ["cb"], writes=[("xc", j)])
            BT, CT = xc[4], xc[5]

            def bcreg(s, eh):
                return bk(3 + eh // 4)[:, (eh % 4) * 128:(eh % 4 + 1) * 128], PK[3 + eh // 4]

            def stage0(c):
                if "nostage0" in DBG:
                    return
                s = c % 2
                tsl = slice(c * 128, (c + 1) * 128)
                hsl = slice(c * 32 + gi * 8, c * 32 + gi * 8 + 8)
                for jx in range(4):
                    sc.tr(bkb(0)[:, jx * 128:(jx + 1) * 128], xc[jx][:, tsl], ident_b[:],
                          reads=[("xc", jx), "identb"], writes=[PK[0]])
                sc.add("dve", lambda e: e.tensor_tensor(
                    out=xdt[s][:, :].rearrange("p (h d) -> p h d", h=8),
                    in0=bkb(0)[:, 0:512].rearrange("p (h d) -> p h d", h=8),
                    in1=dt_t[:, hsl].unsqueeze(2).broadcast_to([128, 8, 64]), op=ALU.mult),
                    reads=[PK[0], "dt_t"], writes=[("xdt", s)])
                sc.add("dve", lambda e: e.tensor_tensor(
                    out=xdw[s][:, :].rearrange("p (h d) -> p h d", h=8),
                    in0=bkb(0)[:, 0:512].rearrange("p (h d) -> p h d", h=8),
                    in1=dtw[:, hsl].unsqueeze(2).broadcast_to([128, 8, 64]), op=ALU.mult),
                    reads=[PK[0], "dtw"], writes=[("xdw", s)])
                sc.tr(bkb(2)[:, 0:128], BT[:, tsl], ident_b[:], reads=[("xc", 4), "identb"], writes=[PK[2]])
                sc.add("dve", lambda e: e.tensor_copy(out=btok[s][:], in_=bkb(2)[:, 0:128]), reads=[PK[2]], writes=[("btok", s)])
                sc.mm(bk(2)[:, 128:256], BT[:, tsl], CT[:, tsl], True, True, reads=[("xc", 4), ("xc", 5)], writes=[PK[2]])
                sc.add("dve", lambda e: e.tensor_tensor(out=cbm[s][:], in0=bk(2)[:, 128:256], in1=tri_f[:], op=ALU.mult),
                       reads=[PK[2], "trif"], writes=[("cbm", s)])

            def bc_mms(c):
                s = c % 2
                for eh in range(8):
                    hcol = c * 32 + gi * 8 + eh
                    reg, rk = bcreg(s, eh)
                    for t3 in range(3):
                        sc.mm(reg, adt3[t3][:, hcol:hcol + 1].broadcast_to([128, 128]), tri_b[:], t3 == 0, t3 == 2,
                              reads=[("adt3", t3), "trib"], writes=[rk])

            def st_mm(c):
                s = c % 2
                if c < NT - 1:
                    sc.mm(bk(1)[:, :], btok[s][:], xdw[s][:], True, True, reads=[("btok", s), ("xdw", s)], writes=[PK[1]])

            def heads(c, part):
                s = c % 2
                tsl = slice(c * 128, (c + 1) * 128)
                if part == 0:
                    for eh in range(1, 8, 2):
                        hcol = c * 32 + gi * 8 + eh
                        reg, rk = bcreg(s, eh)
                        bi = s * 8 + eh
                        sc.add("act", lambda e, reg=reg, bi=bi, hcol=hcol: e.activation(
                            out=argt[bi][:], in_=reg, func=AF.Relu, bias=ack[:, hcol:hcol + 1], scale=-1.0),
                            reads=[rk, "ack"], writes=[("arg", bi)])
                    for eh in range(8):
                        hcol = c * 32 + gi * 8 + eh
                        reg, rk = bcreg(s, eh)
                        bi = s * 8 + eh
                        if eh % 2 == 1:
                            sc.add("act", lambda e, bi=bi: e.activation(out=dect[bi][:], in_=argt[bi][:], func=AF.Exp,
                                                                        scale=-1.0),
                                   reads=[("arg", bi)], writes=[("dec", bi)])
                        else:
                            sc.add("act", lambda e, reg=reg, bi=bi, hcol=hcol: e.activation(
                                out=dect[bi][:], in_=reg, func=AF.Exp, bias=nack[:, hcol:hcol + 1], scale=1.0),
                                reads=[rk, "nack"], writes=[("dec", bi)])
                    if c > 0:
                        for eh in range(8):
                            reg, rk = bcreg(s, eh)
                            bi = s * 8 + eh
                            sc.add("act", lambda e, reg=reg, bi=bi: e.activation(out=Et[bi][:], in_=reg, func=AF.Exp),
                                   reads=[rk], writes=[("E", bi)])
                    return
                if part == 1:
                    for eh in range(8):
                        bi = s * 8 + eh
                        if eh % 2 == 0:
                            sc.add("dve", lambda e, bi=bi: e.scalar_tensor_tensor(
                                out=Mh[bi][:], in0=dect[bi][:], scalar=1.0, in1=cbm[s][:], op0=ALU.min, op1=ALU.mult),
                                reads=[("cbm", s), ("dec", bi)], writes=[("Mh", bi)])
                        else:
                            sc.add("pool", lambda e, bi=bi: e.tensor_tensor(out=Mh[bi][:], in0=cbm[s][:], in1=dect[bi][:],
                                                                          op=ALU.mult),
                                   reads=[("cbm", s), ("dec", bi)], writes=[("Mh", bi)])
                    return
                if part == 2:
                    if c > 0:
                        for eh in range(8):
                            bi = s * 8 + eh
                            sc.add("pool", lambda e, bi=bi: e.tensor_tensor(out=Ct[bi][:], in0=CT[:, tsl], in1=Et[bi][:],
                                                                          op=ALU.mult),
                                   reads=[("xc", 5), ("E", bi)], writes=[("Ct", bi)])
                    return
                for eh in range(8):
                    bi = s * 8 + eh
                    pr, e2 = eh // 2, eh % 2
                    yreg = bk(5)[e2 * 64:(e2 + 1) * 64, pr * 128:(pr + 1) * 128]
                    sc.mm(yreg, xdt[s][:, eh * 64:(eh + 1) * 64], Mh[bi][:], True, c == 0,
                          reads=[("xdt", s), ("Mh", bi)], writes=[YK[pr]])
                    if c > 0:
                        sc.mm(yreg, HTb[(c - 1) % 2][:, eh * 64:(eh + 1) * 64], Ct[bi][:], False, True,
                              reads=[("HTb", (c - 1) % 2), ("Ct", bi)], writes=[YK[pr]])

            def state_upd(c):
                hsl = slice(c * 32 + gi * 8, c * 32 + gi * 8 + 8)
                if c < NT - 1:
                    if c == 0:
                        sc.add("dve", lambda e: e.tensor_copy(out=HT[:], in_=bk(1)[:, :]), reads=[PK[1]], writes=["HT"])
                    else:
                        sc.add("dve", lambda e: e.tensor_tensor(
                            out=HT[:, :].rearrange("p (h d) -> p h d", h=8),
                            in0=HT[:, :].rearrange("p (h d) -> p h d", h=8),
                            in1=cdec[:, hsl].unsqueeze(2).broadcast_to([128, 8, 64]), op=ALU.mult),
                            reads=["HT", "cdec"], writes=["HT"])
                        sc.add("dve", lambda e: e.tensor_tensor(out=HT[:], in0=HT[:], in1=bk(1)[:, :], op=ALU.add),
                               reads=["HT", PK[1]], writes=["HT"])
                    sc.add("act", lambda e: e.copy(out=HTb[c % 2][:], in_=HT[:]), reads=["HT"], writes=[("HTb", c % 2)])

            def tail_a(c):
                g2 = (c % 2) * 4
                tg = c // 4
                zi = (gi * TG + tg) % 2
                cc = c % 4
                tsl = slice(c * 128, (c + 1) * 128)
                hsl = slice(c * 32 + gi * 8, c * 32 + gi * 8 + 8)
                for pr in range(4):
                    dcol = gi * 4 + pr
                    sc.add("dve", lambda e, pr=pr, dcol=dcol: e.scalar_tensor_tensor(
                        out=yv[pr][:], in0=xc[pr][:, tsl], scalar=dsk[:, dcol:dcol + 1], in1=bk(5)[:, pr * 128:(pr + 1) * 128],
                        op0=ALU.mult, op1=ALU.add), reads=[("xc", pr), "dsk", YK[pr]], writes=[("yv", pr)])
                    sc.add("dve", lambda e, pr=pr, g2=g2: e.tensor_tensor(
                        out=yg[g2 + pr][:], in0=yv[pr][:], in1=sz[zi][:, pr, cc * 128:(cc + 1) * 128], op=ALU.mult),
                        reads=[("yv", pr), ("sz", zi)], writes=[("yg", g2 + pr)])
                    sc.add("act", lambda e, pr=pr, g2=g2: e.activation(out=sq[pr][:], in_=yg[g2 + pr][:], func=AF.Square),
                           reads=[("yg", g2 + pr)], writes=[("sq", pr)])

            def tail_a_pe(c):
                for pr in range(4):
                    sc.mm(bk(6)[:, 0:128], ones_b[:], sq[pr][:], pr == 0, pr == 3,
                          reads=["onesb", ("sq", pr)], writes=[PK[6]])

            def tail_a2(c):
                sc.add("act", lambda e: e.activation(out=rbc[:], in_=bk(6)[:, 0:128], func=AF.Ln, bias=epsb[:, 0:1],
                                                   scale=1.0 / 512.0), reads=[PK[6], "epsb"], writes=["rbc"])
                sc.add("act", lambda e: e.activation(out=rbc[:], in_=rbc[:], func=AF.Exp, scale=-0.5),
                       reads=["rbc"], writes=["rbc"])

            def tail_b(c):
                g2 = (c % 2) * 4
                tg = c // 4
                ysi = (gi * TG + tg) % 2
                cc = c % 4
                for pr in range(4):
                    dcol = gi * 4 + pr
                    sc.add("dve", lambda e, pr=pr, dcol=dcol: e.scalar_tensor_tensor(
                        out=yst[ysi][:, pr, cc * 128:(cc + 1) * 128], in0=yg[g2 + pr][:], scalar=snw[:, dcol:dcol + 1],
                        in1=rbc[:], op0=ALU.mult, op1=ALU.mult),
                        reads=[("yg", g2 + pr), "snw", "rbc"], writes=[("yst", ysi)])
                if c % 4 == 3:
                    sc.dma("sp", yssd_v[:, gi * 4:(gi + 1) * 4, tg * 512:(tg + 1) * 512], yst[ysi][:],
                           reads=[("yst", ysi)], writes=[("yssdT", gi * 4 + k) for k in range(4)])

            stage0(0)
            bc_mms(0)
            heads(0, 0)
            heads(0, 1)
            bc_mms(1)
            heads(1, 0)
            for c in range(NT):
                tg = c // 4
                zi = (gi * TG + tg) % 2
                if c % 4 == 0:
                    sc.dma("sp", sz[zi][:], projT_v[:, gi * 4:(gi + 1) * 4, tg * 512:(tg + 1) * 512],
                           reads=PJ(R_Z + gi * 512, 512), writes=[("sz", zi)])
                st_mm(c)
                state_upd(c)
                if c + 1 < NT:
                    heads(c + 1, 2)
                heads(c, 3)
                tail_a(c)
                if c + 1 < NT:
                    stage0(c + 1)
                if c + 2 < NT:
                    bc_mms(c + 2)
                    heads(c + 2, 0)
                tail_a_pe(c)
                if c + 1 < NT:
                    heads(c + 1, 1)
                if c > 0:
                    tail_b(c - 1)
                tail_a2(c)
            tail_b(NT - 1)
        sc.pop()
        if stop_after == f"ssd{L}":
            break

        sc.push()
        ys = sc.tile([128, 16, S], BF16, "ys")
        ya = sc.tile([128, 8, S], BF16, "ya")
        mg = sc.tile([128, 8, S], BF16, "mg")
        wo = sc.tile([128, 8, D], BF16, "wo")
        for k in range(16):
            sc.dma("sp", ys[:, k, :], yssdT[k * 128:(k + 1) * 128, :], reads=[("yssdT", k)], writes=[("ys", k)])
        for k in range(8):
            sc.dma("sp", ya[:, k, :], yattnT[k * 128:(k + 1) * 128, :], reads=[("yattnT", k)], writes=[("ya", k)])
        for c in range(8):
            load_w("pool", wo[:, c, :], w_o[L, c * 128:(c + 1) * 128, :], ("wo", c))
        sc.push()
        was = [sc.tile([128, 16, 128], BF16, "wa") for _ in range(2)]
        wbs = [sc.tile([128, 8, 128], BF16, "wb") for _ in range(2)]
        gss = [sc.tile([128, S], BF16, "gs") for _ in range(2)]
        gas = [sc.tile([128, S], BF16, "ga") for _ in range(2)]
        m1 = [sc.tile([128, 512], F32, "m1") for _ in range(2)]
        m2 = [sc.tile([128, 512], F32, "m2") for _ in range(2)]
        w_a_v = w_a[L].rearrange("(k p) n -> p k n", p=128)
        w_b_v = w_b[L].rearrange("(k p) n -> p k n", p=128)
        YSK = [("ys", k) for k in range(16)]
        YAK = [("ya", k) for k in range(8)]
        for oc in range(8):
            i2 = oc % 2
            load_w("pool", was[i2][:], w_a_v[:, :, oc * 128:(oc + 1) * 128], ("wa", i2))
            load_w("pool", wbs[i2][:], w_b_v[:, :, oc * 128:(oc + 1) * 128], ("wb", i2))
            r0 = R_GS + oc * 128
            sc.dma("sp", gss[i2][:], projT[r0:r0 + 128, :], reads=PJ(r0), writes=[("gs", i2)])
            r0 = R_GA + oc * 128
            sc.dma("sp", gas[i2][:], projT[r0:r0 + 128, :], reads=PJ(r0), writes=[("ga", i2)])
            for tg in range(TG):
                sl = slice(tg * 512, (tg + 1) * 512)
                pa, pb = 0 + (tg % 2), 2 + (tg % 2)
                for k in range(16):
                    sc.mm(bk(pa)[:, :], was[i2][:, k, :], ys[:, k, sl], k == 0, k == 15, reads=[("wa", i2), YSK[k]],
                          writes=[PK[pa]])
                for k in range(8):
                    sc.mm(bk(pb)[:, :], wbs[i2][:, k, :], ya[:, k, sl], k == 0, k == 7, reads=[("wb", i2), YAK[k]],
                          writes=[PK[pb]])
                t = tg % 2
                sc.add("dve", lambda e, t=t, pa=pa, i2=i2, sl=sl: e.tensor_tensor(out=m1[t][:], in0=bk(pa)[:, :],
                                                                                in1=gss[i2][:, sl], op=ALU.mult),
                       reads=[PK[pa], ("gs", i2)], writes=[("m1", t)])
                sc.add("dve", lambda e, t=t, pb=pb, i2=i2, sl=sl: e.tensor_tensor(out=m2[t][:], in0=bk(pb)[:, :],
                                                                                in1=gas[i2][:, sl], op=ALU.mult),
                       reads=[PK[pb], ("ga", i2)], writes=[("m2", t)])
                sc.add("dve", lambda e, t=t, oc=oc, sl=sl: e.tensor_tensor(out=mg[:, oc, sl], in0=m1[t][:], in1=m2[t][:],
                                                                          op=ALU.add),
                       reads=[("m1", t), ("m2", t)], writes=[("mg", oc)])
        sc.pop()
        hts = [sc.tile([128, D], F32, "hres") for _ in range(2)]
        MGK = [("mg", k) for k in range(8)]
        WOK = [("wo", k) for k in range(8)]
        for tb in range(NT):
            hb = tb % 2
            hk = ("hres", hb)
            sc.dma("sp", hts[hb][:], hD[tb * 128:(tb + 1) * 128, :], reads=[("h", tb)], writes=[hk])
            for half in range(2):
                pb = 4 + half
                for c in range(8):
                    sc.mm(bk(pb)[:, :], mg[:, c, tb * 128:(tb + 1) * 128], wo[:, c, half * 512:(half + 1) * 512], c == 0,
                          c == 7, reads=[MGK[c], WOK[c]], writes=[PK[pb]])
                sc.add("dve", lambda e, hb=hb, half=half, pb=pb: e.tensor_tensor(
                    out=hts[hb][:, half * 512:(half + 1) * 512], in0=hts[hb][:, half * 512:(half + 1) * 512],
                    in1=bk(pb)[:, :], op=ALU.add), reads=[hk, PK[pb]], writes=[hk])
            sc.dma("sp", hD[tb * 128:(tb + 1) * 128, :], hts[hb][:], reads=[hk], writes=[("h", tb)])
        sc.pop()
        if stop_after == f"merge{L}":
            break

        sc.push()
        uT = sc.tile([128, 8, S], BF16, "uT2")
        norm_to_uT(uT, nffn[L], f"n{L}b")
        wd = sc.tile([128, NJ, D], BF16, "wd")
        for j in range(NJ):
            load_w("pool", wd[:, j, :], w_d[L, j * 128:(j + 1) * 128, :], ("wd", j))
        WDK = [("wd", j) for j in range(NJ)]
        actT = sc.tile([128, NJ, 1024], BF16, "actT")
        wgs = [sc.tile([128, 8, 128], BF16, "wg") for _ in range(2)]
        wus = [sc.tile([128, 8, 128], BF16, "wu") for _ in range(2)]
        sgt = [sc.tile([128, 512], F32, "sg") for _ in range(2)]
        hts = [sc.tile([128, D], F32, "hres2") for _ in range(2)]
        w_gu_v = w_gu[L].rearrange("(c p) n -> p c n", p=128)
        AK = [("actT", j) for j in range(NJ)]
        nw_ = 0
        for th in range(2):
            for j in range(NJ):
                i2 = nw_ % 2
                nw_ += 1
                load_w("pool", wgs[i2][:], w_gu_v[:, :, j * 128:(j + 1) * 128], ("wg", i2))
                load_w("pool", wus[i2][:], w_gu_v[:, :, FFN + j * 128:FFN + (j + 1) * 128], ("wu", i2))
                for t2 in range(2):
                    tg = th * 2 + t2
                    sl = slice(tg * 512, (tg + 1) * 512)
                    pg, pu = 0 + t2, 2 + t2
                    for c in range(8):
                        sc.mm(bk(pg)[:, :], wgs[i2][:, c, :], uT[:, c, sl], c == 0, c == 7, reads=[("wg", i2), UK[tg]],
                              writes=[PK[pg]])
                    for c in range(8):
                        sc.mm(bk(pu)[:, :], wus[i2][:, c, :], uT[:, c, sl], c == 0, c == 7, reads=[("wu", i2), UK[tg]],
                              writes=[PK[pu]])
                    sc.add("act", lambda e, t2=t2, pg=pg: e.activation(out=sgt[t2][:], in_=bk(pg)[:, :], func=AF.Silu),
                           reads=[PK[pg]], writes=[("sg", t2)])
                    sc.add("dve", lambda e, t2=t2, pu=pu, j=j: e.tensor_tensor(
                        out=actT[:, j, t2 * 512:(t2 + 1) * 512], in0=sgt[t2][:], in1=bk(pu)[:, :], op=ALU.mult),
                        reads=[("sg", t2), PK[pu]], writes=[AK[j]])
            for t8 in range(8):
                tb = th * 8 + t8
                hb = tb % 2
                hk = ("hres2", hb)
                sc.dma("sp", hts[hb][:], hD[tb * 128:(tb + 1) * 128, :], reads=[("h", tb)], writes=[hk])
                for half in range(2):
                    pb = 4 + half
                    for j in range(NJ):
                        sc.mm(bk(pb)[:, :], actT[:, j, t8 * 128:(t8 + 1) * 128], wd[:, j, half * 512:(half + 1) * 512],
                              j == 0, j == NJ - 1, reads=[AK[j], WDK[j]], writes=[PK[pb]])
                    sc.add("dve", lambda e, hb=hb, half=half, pb=pb: e.tensor_tensor(
                        out=hts[hb][:, half * 512:(half + 1) * 512], in0=hts[hb][:, half * 512:(half + 1) * 512],
                        in1=bk(pb)[:, :], op=ALU.add), reads=[hk, PK[pb]], writes=[hk])
                sc.dma("sp", hD[tb * 128:(tb + 1) * 128, :], hts[hb][:], reads=[hk], writes=[("h", tb)])
        sc.pop()
        if stop_after == f"ffn{L}":
            break

    sc.sb_stack = []
    sc.push()
    nf = sc.tile([128, D], F32, "nf")
    sc.dma("sp", nf[:], nfin[:, :], writes=["nf"])
    hall = sc.tile([128, NT, D], F32, "hallf")
    ots = [sc.tile([128, D], F32, "ofin") for _ in range(3)]
    junk = sc.tile([128, D], BF16, "junkf")
    st = sc.tile([128, 2 * NT], F32, "fstat")
    OUTK = []
    for tb in range(NT):
        sc.dma("sp", hall[:, tb, :], hD[tb * 128:(tb + 1) * 128, :], reads=[("h", tb)], writes=[("hallf", tb)])
        sc.add("act", lambda e, tb=tb: e.activation(out=junk[:], in_=hall[:, tb, :], func=AF.Square,
                                                  accum_out=st[:, tb:tb + 1]),
               reads=[("hallf", tb)], writes=["junkf", ("fss", tb)])
    sc.add("act", lambda e: e.activation(out=st[:, NT:2 * NT], in_=st[:, 0:NT], func=AF.Ln, bias=epsb[:, 0:1],
                                       scale=1.0 / D), reads=[("fss", tb) for tb in range(NT)] + ["epsb"], writes=["frstd"])
    sc.add("act", lambda e: e.activation(out=st[:, NT:2 * NT], in_=st[:, NT:2 * NT], func=AF.Exp, scale=-0.5),
           reads=["frstd"], writes=["frstd"])
    for tb in range(NT):
        b = tb % 3
        sc.add("act", lambda e, b=b, tb=tb: e.activation(out=ots[b][:], in_=hall[:, tb, :], func=AF.Identity,
                                                       scale=st[:, NT + tb:NT + tb + 1]),
               reads=[("hallf", tb), "frstd"], writes=[("ofin", b)])
        sc.add("dve", lambda e, b=b: e.tensor_tensor(out=ots[b][:], in0=ots[b][:], in1=nf[:], op=ALU.mult),
               reads=[("ofin", b), "nf"], writes=[("ofin", b)])
        sc.dma("sp", out[tb * 128:(tb + 1) * 128, :], ots[b][:], reads=[("ofin", b)], writes=[("out", tb)])
        OUTK.append(("out", tb))
    allk = list(OUTK) + [k for k in sc.lastw.keys() if isinstance(k, tuple) and k[0] in ("projT", "yssdT", "yattnT", "h")]
    sc.add("sp", lambda e: e.nop(), reads=allk)
    sc.pop()

    sems = {}
    for e in Sched.ENGS:
        sems[e] = nc.alloc_semaphore(f"sem_{e}")
    dsems = [nc.alloc_semaphore(f"dsem{i}") for i in range(NDSEM)]
    stats = sc.emit(sems, dsems)
    return stats


def host_constants():
    a = np.arange(128)
    dab = a[None, :] - a[:, None]
    ge = (dab >= 0)
    le = (dab <= 0)

    def m(cond):
        return cond.astype(np.float32)
    g0 = np.concatenate([m(le), m(ge)], axis=1)
    mod4 = (dab % 4 == 0)
    g1 = np.concatenate([m(le & mod4)] + [m(mod4)] * 3 + [m(ge & mod4)], axis=1)
    mod16 = (dab % 16 == 0)
    g2 = np.concatenate([m(mod16)] * 15 + [m(ge & mod16)], axis=1)
    amask = np.concatenate([g0, g1, g2], axis=1)
    assert amask.shape == (128, 2944)
    tri = (a[:, None] <= a[None, :]).astype(np.float32)
    ident = np.eye(128, dtype=np.float32)
    half = 16
    inv = 500000.0 ** (-np.arange(0, 32, 2, dtype=np.float64) / 32.0)
    ang = np.arange(S, dtype=np.float64)[None, :] * inv[:, None]
    cos, sin = np.cos(ang), np.sin(ang)
    C32 = np.concatenate([cos, cos], axis=0)
    S32 = np.concatenate([-sin, sin], axis=0)
    rope = np.concatenate([C32, S32], axis=1).astype(np.float32)
    rsw = np.zeros((32, 32), np.float32)
    for mm_ in range(32):
        rsw[(mm_ + half) % 32, mm_] = 1.0
    return dict(c_amask=amask, c_tri=tri, c_ident=ident, c_rope=rope, c_rsw=rsw)


def host_layout(inputs):
    f = lambda a: np.ascontiguousarray(np.asarray(a, dtype=np.float32))
    m = {}
    for n in ("w_in", "w_ssd_branch", "w_attn_branch", "w_out", "w_gate_up", "w_down"):
        m[n] = f(inputs[n])
    m["nmix"] = f(np.asarray(inputs["norm_mix"]).reshape(DEPTH, 8, 128).transpose(0, 2, 1))
    m["nffn"] = f(np.asarray(inputs["norm_ffn"]).reshape(DEPTH, 8, 128).transpose(0, 2, 1))
    m["nfin"] = f(np.broadcast_to(np.asarray(inputs["norm_final"])[None, :], (128, D)))
    cw = np.asarray(inputs["conv_w"])
    m["convw"] = f(cw.transpose(0, 2, 1).reshape(DEPTH, 24, 128, 4).transpose(0, 2, 1, 3).reshape(DEPTH, 128, 96))
    m["convb"] = f(np.asarray(inputs["conv_b"]).reshape(DEPTH, 24, 128).transpose(0, 2, 1))
    m["dtb"] = f(np.broadcast_to(np.asarray(inputs["dt_bias"])[:, None, :], (DEPTH, 128, 32)))
    m["alog"] = f(np.broadcast_to(np.asarray(inputs["a_log"])[:, None, :], (DEPTH, 128, 32)))
    m["dskip"] = f(np.repeat(np.asarray(inputs["d_skip"]), 64, axis=1).reshape(DEPTH, 16, 128).transpose(0, 2, 1))
    m["ssdn"] = f(np.asarray(inputs["ssd_norm"]).reshape(DEPTH, 16, 128).transpose(0, 2, 1))
    m.update(host_constants())
    return m


_CACHE = {}


def kernel(**inputs):
    x = np.asarray(inputs["x"], dtype=np.float32)
    B = x.shape[0]
    shared = host_layout(inputs)
    if "nc" not in _CACHE:
        nc = bass.Bass("TRN2", target_bir_lowering=False)
        build(nc)
        _CACHE["nc"] = nc
    nc = _CACHE["nc"]
    in_maps = []
    for b in range(B):
        d = dict(shared)
        d["x"] = np.ascontiguousarray(x[b])
        in_maps.append(d)
    res = run_bass_kernel_spmd(nc, in_maps, core_ids=list(range(B)))
    return np.stack([np.asarray(r["out"], dtype=np.float32) for r in res.results], axis=0)
```

```python
import types
import numpy as np
import concourse.bass as bass
import concourse.mybir as mybir
from concourse.bass_utils import run_bass_kernel_spmd

F32 = mybir.dt.float32
BF16 = mybir.dt.bfloat16
AF = mybir.ActivationFunctionType
ALU = mybir.AluOpType

S = 2048
D = 1024
DEPTH = 2
NT = 16
TG = 4
N_IN = 12320
SSD_INNER = 2048
FFN = 2816
NJ = FFN // 128
EPS = 1e-5
R_Z = 0
R_XBC = 2048
R_Q = 5120
R_K = 8192
R_V = 9216
R_GS = 10240
R_GA = 11264
NPROJ = 12288
C_DT = 5120


def col_of_row(r):
    return r if r < 5120 else r + 32


import os
DBG = os.environ.get('KDBG', '').split(',')
NDSEM = 40
PIPE_SSD = True
SBUF_BASE = 16640
SBUF_END = 229376 - 128


def _freeze(fn):
    if fn.__closure__ is None:
        return fn
    cells = []
    for c in fn.__closure__:
        try:
            cells.append(types.CellType(c.cell_contents))
        except ValueError:
            cells.append(c)
    return types.FunctionType(fn.__code__, fn.__globals__, fn.__name__, fn.__defaults__, tuple(cells))


class Op:
    __slots__ = ("eng", "fn", "deps", "need_inc", "is_dma", "dslot", "dval", "done")

    def __init__(self, eng, fn, is_dma):
        self.eng = eng
        self.fn = fn
        self.deps = ()
        self.need_inc = False
        self.is_dma = is_dma
        self.dslot = -1
        self.dval = 0
        self.done = None


class Sched:
    ENGS = ("pe", "act", "dve", "pool", "sp")

    def __init__(self, nc):
        self.nc = nc
        self.ops = {e: [] for e in self.ENGS}
        self.lastw = {}
        self.readers = {}
        self.ndma = 0
        self.sb_off = SBUF_BASE
        self.sb_stack = []
        self.nalloc = 0
        self.dma_since = []
        self.last_compute = {}

    def push(self):
        self.sb_stack.append(self.sb_off)

    def pop(self):
        self.sb_off = self.sb_stack.pop()
        self.barrier()

    def barrier(self):
        targets = set(self.dma_since)
        for e, i in self.last_compute.items():
            targets.add((e, i))
        for eng in self.ENGS:
            op = Op(eng, lambda e: e.nop(), False)
            idx = len(self.ops[eng])
            deps = {d for d in targets if not (d[0] == eng and not self.ops[d[0]][d[1]].is_dma)}
            op.deps = deps
            for d in deps:
                self.ops[d[0]][d[1]].need_inc = True
            self.ops[eng].append(op)
        self.dma_since = []

    def tile(self, shape, dtype, name=None):
        esz = 4 if dtype == F32 else 2
        n = 1
        for d in shape[1:]:
            n *= d
        nbytes = (n * esz + 63) // 64 * 64
        off = self.sb_off
        assert off + nbytes <= SBUF_END, f"SBUF overflow {name} {off + nbytes}"
        self.sb_off = off + nbytes
        self.nalloc += 1
        nm = f"{name or 't'}_{self.nalloc}"
        return self.nc.alloc_sbuf_tensor_at(nm, list(shape), dtype, offset=off)

    def add(self, eng, fn, reads=(), writes=(), dma=False):
        op = Op(eng, _freeze(fn), dma)
        idx = len(self.ops[eng])
        self.ops[eng].append(op)
        me = (eng, idx)
        deps = set()
        psr = [k for k in reads if isinstance(k, tuple) and k[0] == "ps"]
        if psr:
            reads = [k for k in reads if not (isinstance(k, tuple) and k[0] == "ps")]
            writes = list(writes) + [k for k in psr if k not in writes]
        for k in reads:
            w = self.lastw.get(k)
            if w is not None:
                deps.add(w)
        for k in writes:
            w = self.lastw.get(k)
            if w is not None:
                deps.add(w)
            for r in self.readers.get(k, ()):
                deps.add(r)
        deps.discard(me)
        if eng == "pe":
            deps = {d for d in deps if not (d[0] == "pe")}
        op.deps = deps
        for d in deps:
            self.ops[d[0]][d[1]].need_inc = True
        for k in reads:
            self.readers.setdefault(k, []).append(me)
        for k in writes:
            self.lastw[k] = me
            self.readers[k] = []
        if dma:
            op.dslot = self.ndma % NDSEM
            op.dval = 16 * (self.ndma // NDSEM + 1)
            self.ndma += 1
            self.dma_since.append(me)
        else:
            self.last_compute[eng] = idx
        return op

    def dma(self, eng, out, in_, reads=(), writes=()):
        return self.add(eng, lambda e: e.dma_start(out=out, in_=in_), reads, writes, dma=True)

    def mm(self, out, lhsT, rhs, start, stop, reads=(), writes=()):
        return self.add("pe", lambda e: e.matmul(out, lhsT=lhsT, rhs=rhs, start=start, stop=stop), reads, writes)

    def tr(self, out, in_, ident, reads=(), writes=()):
        return self.add("pe", lambda e: e.transpose(out, in_, ident), reads, writes)

    def emit(self, sems, dsems):
        for eng in self.ENGS:
            cnt = 0
            for op in self.ops[eng]:
                if op.is_dma:
                    op.done = (("d", op.dslot), op.dval)
                elif op.need_inc:
                    cnt += 1
                    op.done = ((eng,), cnt)
        semof = {}
        for e in self.ENGS:
            semof[(e,)] = sems[e]
        for i in range(NDSEM):
            semof[("d", i)] = dsems[i]
        nc = self.nc
        stats = {}

        def run(eng, e):
            waited = {}
            nw = 0
            for op in self.ops[eng]:
                want = {}
                for d in op.deps:
                    sk, v = self.ops[d[0]][d[1]].done
                    if want.get(sk, 0) < v:
                        want[sk] = v
                if op.is_dma and op.dval > 16:
                    sk = ("d", op.dslot)
                    if want.get(sk, 0) < op.dval - 16:
                        want[sk] = op.dval - 16
                for sk, v in want.items():
                    if waited.get(sk, 0) < v:
                        e.wait_ge(semof[sk], v)
                        waited[sk] = v
                        nw += 1
                ins = op.fn(e)
                if op.is_dma:
                    ins.then_inc(semof[("d", op.dslot)], 16)
                elif op.need_inc:
                    ins.then_inc(semof[(eng,)], 1)
            stats[eng] = (len(self.ops[eng]), nw)

        with nc.Block() as block:
            @block.tensor
            def _(e):
                run("pe", e)

            @block.scalar
            def _(e):
                run("act", e)

            @block.vector
            def _(e):
                run("dve", e)

            @block.gpsimd
            def _(e):
                run("pool", e)

            @block.sync
            def _(e):
                run("sp", e)
        return stats


def build(nc, n_layers=DEPTH, dbg=False, stop_after=None):
    def din(name, shape, dt=F32):
        return nc.dram_tensor(name, list(shape), dt, kind="ExternalInput").ap()

    x = din("x", [S, D])
    w_in = din("w_in", [DEPTH, D, N_IN])
    w_a = din("w_ssd_branch", [DEPTH, SSD_INNER, D])
    w_b = din("w_attn_branch", [DEPTH, D, D])
    w_o = din("w_out", [DEPTH, D, D])
    w_gu = din("w_gate_up", [DEPTH, D, 2 * FFN])
    w_d = din("w_down", [DEPTH, FFN, D])
    nmix = din("nmix", [DEPTH, 128, 8])
    nffn = din("nffn", [DEPTH, 128, 8])
    nfin = din("nfin", [128, D])
    convw = din("convw", [DEPTH, 128, 24 * 4])
    convb = din("convb", [DEPTH, 128, 24])
    dtb = din("dtb", [DEPTH, 128, 32])
    alog = din("alog", [DEPTH, 128, 32])
    dskip = din("dskip", [DEPTH, 128, 16])
    ssdn = din("ssdn", [DEPTH, 128, 16])
    c_amask = din("c_amask", [128, 2944])
    c_tri = din("c_tri", [128, 128])
    c_ident = din("c_ident", [128, 128])
    c_rope = din("c_rope", [32, 2 * S])
    c_rsw = din("c_rsw", [32, 32])
    out = nc.dram_tensor("out", [S, D], F32, kind="ExternalOutput").ap()

    def dscr(name, shape, dt):
        if dbg:
            return nc.dram_tensor(name, list(shape), dt, kind="ExternalOutput").ap()
        return nc.dram_tensor(name, list(shape), dt, kind="Internal").ap()

    hD = dscr("h_scr", [S, D], F32)
    projT = dscr("projT", [NPROJ, S], BF16)
    yssdT = dscr("yssdT", [SSD_INNER, S], BF16)
    yattnT = dscr("yattnT", [D, S], BF16)

    sc = Sched(nc)
    banks = [nc.alloc_psum_tensor(f"bank{i}", [128, 512], F32) for i in range(8)]

    def bk(i):
        return banks[i]

    def bkb(i):
        return banks[i][:, :].bitcast(BF16)

    PK = [("ps", i) for i in range(8)]

    ident_f = sc.tile([128, 128], F32, "identf")
    ident_b = sc.tile([128, 128], BF16, "identb")
    ones_f = sc.tile([128, 128], F32, "onesf")
    tri_f = sc.tile([128, 128], F32, "trif")
    amask = sc.tile([128, 2944], BF16, "amask")
    ropeC = sc.tile([32, S], F32, "ropeC")
    ropeS = sc.tile([32, S], F32, "ropeS")
    rsw = sc.tile([32, 32], BF16, "rsw")
    dtraw = sc.tile([128, 512], F32, "dtraw")
    epsb = sc.tile([128, 1], F32, "epsb")
    sc.dma("sp", ident_f[:], c_ident[:, :], writes=["identf"])
    sc.dma("pool", ident_b[:], c_ident[:, :], writes=["identb"])
    sc.dma("sp", tri_f[:], c_tri[:, :], writes=["trif"])
    sc.dma("pool", amask[:], c_amask[:, :], writes=["amask"])
    sc.dma("sp", ropeC[:], c_rope[:, 0:S], writes=["ropeC"])
    sc.dma("sp", ropeS[:], c_rope[:, S:2 * S], writes=["ropeS"])
    sc.dma("pool", rsw[:], c_rsw[:, :], writes=["rsw"])
    tri_b = sc.tile([128, 128], BF16, "trib")
    sc.dma("pool", tri_b[:], c_tri[:, :], writes=["trib"])
    sc.add("pool", lambda e: e.memset(ones_f[:], 1.0), writes=["onesf"])
    ones_b = sc.tile([128, 128], BF16, "onesb")
    sc.add("pool", lambda e: e.memset(ones_b[:], 1.0), writes=["onesb"])
    sc.add("pool", lambda e: e.memset(epsb[:], EPS), writes=["epsb"])

    HK = [("h", tb) for tb in range(NT)]

    def rms_stats(ht, htk, ss_col, rstd_col, tmpk, sqjunk, sqk, n_feat):
        sc.add("act", lambda e: e.activation(out=sqjunk, in_=ht, func=AF.Square, accum_out=ss_col),
               reads=[htk], writes=[sqk, tmpk])
        sc.add("act", lambda e: e.activation(out=rstd_col, in_=ss_col, func=AF.Sqrt, bias=epsb[:, 0:1], scale=1.0 / n_feat),
               reads=[tmpk, "epsb"], writes=[tmpk])
        sc.add("dve", lambda e: e.reciprocal(out=rstd_col, in_=rstd_col), reads=[tmpk], writes=[tmpk])

    def norm_to_uT(uT, nw_dram, tagp, src=None):
        srcD = hD if src is None else src
        sc.push()
        nw = sc.tile([128, 8], F32, "nw")
        sc.dma("sp", nw[:], nw_dram, writes=[tagp + "nw"])
        hall = sc.tile([128, NT, D], F32, "hall")
        ubs = [sc.tile([128, D], BF16, "ub") for _ in range(4)]
        junk = sc.tile([128, D], BF16, "junk")
        st = sc.tile([128, 2 * NT], F32, "nstat")
        for tb in range(NT):
            rk = [("h", tb)] if src is None else []
            sc.dma("sp", hall[:, tb, :], srcD[tb * 128:(tb + 1) * 128, :], reads=rk, writes=[(tagp + "hall", tb)])
            sc.add("act", lambda e, tb=tb: e.activation(out=junk[:], in_=hall[:, tb, :], func=AF.Square,
                                                      accum_out=st[:, tb:tb + 1]),
                   reads=[(tagp + "hall", tb)], writes=[tagp + "junk", (tagp + "ss", tb)])
        SSK = [(tagp + "ss", tb) for tb in range(NT)]
        sc.add("act", lambda e: e.activation(out=st[:, NT:2 * NT], in_=st[:, 0:NT], func=AF.Ln, bias=epsb[:, 0:1],
                                           scale=1.0 / D), reads=SSK + ["epsb"], writes=[tagp + "rstd"])
        sc.add("act", lambda e: e.activation(out=st[:, NT:2 * NT], in_=st[:, NT:2 * NT], func=AF.Exp, scale=-0.5),
               reads=[tagp + "rstd"], writes=[tagp + "rstd"])
        for tb in range(NT):
            b = tb % 4
            ub = ubs[b]
            ubk = (tagp + "ub", b)
            sc.add("act", lambda e, ub=ub, tb=tb: e.activation(out=ub[:], in_=hall[:, tb, :], func=AF.Identity,
                                                             scale=st[:, NT + tb:NT + tb + 1]),
                   reads=[(tagp + "hall", tb), tagp + "rstd"], writes=[ubk])
            pb = tb % 4
            for c in range(8):
                sc.tr(bkb(pb)[:, c * 128:(c + 1) * 128], ub[:, c * 128:(c + 1) * 128], ident_b[:],
                      reads=[ubk, "identb"], writes=[PK[pb]])
            sc.add("dve", lambda e, pb=pb, tb=tb: e.tensor_tensor(
                out=uT[:, :, tb * 128:(tb + 1) * 128],
                in0=bkb(pb)[:, :].rearrange("p (c t) -> p c t", c=8),
                in1=nw[:, :].unsqueeze(2).broadcast_to([128, 8, 128]), op=ALU.mult),
                reads=[PK[pb], tagp + "nw"], writes=[("uT", tb // 4)])
        sc.pop()

    def load_w(eng, dst, src, key):
        sc.dma(eng, dst, src, writes=[key])

    for L in range(n_layers):
        sc.push()
        uT = sc.tile([128, 8, S], BF16, "uT")
        norm_to_uT(uT, nmix[L], f"n{L}a", src=(x if L == 0 else None))
        if L == 0:
            sc.dma("sp", hD[:, :], x[:, :], writes=HK)
        UK = [("uT", g) for g in range(TG)]

        sc.push()
        wts = [sc.tile([128, 8, 128], BF16, "wt") for _ in range(3)]
        stg = [sc.tile([128, S], BF16, "stg") for _ in range(2)]
        w_in_v = w_in[L].rearrange("(c p) n -> p c n", p=128)
        for ch in range(NPROJ // 128):
            r0 = ch * 128
            c0 = col_of_row(r0)
            wt = wts[ch % 3]
            wk = ("ipw", ch % 3)
            load_w("pool", wt[:], w_in_v[:, :, c0:c0 + 128], wk)
            sg = stg[ch % 2]
            sk = ("ipstg", ch % 2)
            if r0 < R_XBC:
                fn = AF.Silu
            elif r0 >= R_GS:
                fn = AF.Sigmoid
            else:
                fn = None
            for tg in range(TG):
                pb = (ch * TG + tg) % 4
                for c in range(8):
                    sc.mm(bk(pb)[:, :], wt[:, c, :], uT[:, c, tg * 512:(tg + 1) * 512], c == 0, c == 7,
                          reads=[wk, UK[tg]], writes=[PK[pb]])
                dst = sg[:, tg * 512:(tg + 1) * 512]
                if fn is not None:
                    sc.add("act", lambda e, dst=dst, pb=pb, fn=fn: e.activation(out=dst, in_=bk(pb)[:, :], func=fn),
                           reads=[PK[pb]], writes=[sk])
                elif tg % 2 == 0:
                    sc.add("act", lambda e, dst=dst, pb=pb: e.copy(out=dst, in_=bk(pb)[:, :]),
                           reads=[PK[pb]], writes=[sk])
                else:
                    sc.add("dve", lambda e, dst=dst, pb=pb: e.tensor_copy(out=dst, in_=bk(pb)[:, :]),
                           reads=[PK[pb]], writes=[sk])
            sc.dma("sp", projT[r0:r0 + 128, :], sg[:], reads=[sk], writes=[("projT", ch)])
        wdt = sc.tile([128, 8, 32], BF16, "wdt")
        load_w("pool", wdt[:], w_in_v[:, :, C_DT:C_DT + 32], "wdt")
        for tb in range(NT):
            for c in range(8):
                sc.mm(bk(4)[:, tb * 32:(tb + 1) * 32], uT[:, c, tb * 128:(tb + 1) * 128], wdt[:, c, :], c == 0, c == 7,
                      reads=["wdt", UK[tb // 4]], writes=[PK[4]])
        sc.add("dve", lambda e: e.tensor_copy(out=dtraw[:], in_=bk(4)[:, :]), reads=[PK[4]], writes=["dtraw"])
        sc.pop()
        sc.pop()
        if stop_after == f"inproj{L}":
            break

        def PJ(r0, n=128):
            return [("projT", r // 128) for r in range(r0, r0 + n, 128)]

        sc.push()
        NB = 2
        qk = [[sc.tile([128, S], BF16, "qk") for _ in range(4)] for _ in range(NB)]
        vT = [sc.tile([128, S], BF16, "vT") for _ in range(NB)]
        V1 = [sc.tile([128, NT, 129], BF16, "V1") for _ in range(NB)]
        t1 = sc.tile([32, S], F32, "ropet1")
        t2s = [sc.tile([32, 512], F32, "ropet2") for _ in range(2)]
        NPT = 8
        NSB = 4
        LA = 3
        PTs = [sc.tile([128, 512], BF16, "PT") for _ in range(NPT)]
        ytok = [sc.tile([128, 512], BF16, "ytok") for _ in range(2)]
        ystg = [sc.tile([128, S], BF16, "aystg") for _ in range(2)]
        rden = sc.tile([128, NT], F32, "rden")
        for b in range(NB):
            sc.add("pool", lambda e, b=b: e.memset(V1[b][:], 1.0), writes=[("V1", b)])
        scale = float(128 ** -0.5)
        GOFF = (0, 256, 896)
        GLEN = (2, 5, 16)

        def attn_prologue(hd):
            b = hd % NB
            QK = [("qk", b, i) for i in range(4)]
            for g in range(3):
                r0 = R_Q + g * 1024 + hd * 128
                sc.dma("sp", qk[b][g][:], projT[r0:r0 + 128, :], reads=PJ(r0), writes=[QK[g]])
            r0 = R_K + hd * 128
            sc.dma("sp", qk[b][3][:], projT[r0:r0 + 128, :], reads=PJ(r0), writes=[QK[3]])
            r0 = R_V + hd * 128
            sc.dma("sp", vT[b][:], projT[r0:r0 + 128, :], reads=PJ(r0), writes=[("vT", b)])
            yield
            for i in range(4):
                qt = qk[b][i]
                sc.add("dve", lambda e, qt=qt: e.tensor_tensor(out=t1[:], in0=qt[0:32, :], in1=ropeC[:], op=ALU.mult),
                       reads=[QK[i], "ropeC"], writes=["ropet1"])
                yield
                for tg in range(TG):
                    sl = slice(tg * 512, (tg + 1) * 512)
                    pb = 6 + (tg % 2)
                    t2 = t2s[tg % 2]
                    t2k = ("ropet2", tg % 2)
                    sc.mm(bk(pb)[0:32, :], rsw[:, :], qt[0:32, sl], True, True, reads=["rsw", QK[i]], writes=[PK[pb]])
                    sc.add("dve", lambda e, pb=pb, sl=sl, t2=t2: e.tensor_tensor(out=t2[:], in0=bk(pb)[0:32, :],
                                                                               in1=ropeS[:, sl], op=ALU.mult),
                           reads=[PK[pb], "ropeS"], writes=[t2k])
                    sc.add("dve", lambda e, qt=qt, sl=sl, t2=t2: e.tensor_tensor(out=qt[0:32, sl], in0=t1[:, sl], in1=t2[:],
                                                                               op=ALU.add),
                           reads=["ropet1", t2k], writes=[QK[i]])
                    yield
            for half in range(2):
                pb = 6 + half
                for j in range(8):
                    tb = half * 8 + j
                    sc.tr(bkb(pb)[:, j * 128:(j + 1) * 128], vT[b][:, tb * 128:(tb + 1) * 128], ident_b[:],
                          reads=[("vT", b), "identb"], writes=[PK[pb]])
                sc.add("act", lambda e, pb=pb, half=half, b=b: e.copy(
                    out=V1[b][:, half * 8:(half + 1) * 8, 0:128],
                    in_=bkb(pb)[:, :].rearrange("p (j d) -> p j d", j=8)),
                    reads=[PK[pb]], writes=[("V1", b)])
                yield

        def attn_core(hd, gen):
            b = hd % NB
            QK = [("qk", b, i) for i in range(4)]
            kT = qk[b][3]
            runs_all = []
            for qb in range(NT):
                pairs = []
                for g, nprev in enumerate((1, 4, qb)):
                    lo = max(0, qb - nprev)
                    for kb in range(lo, qb + 1):
                        mi = GLEN[g] - 1 - (qb - kb)
                        pairs.append((g, kb, GOFF[g] + mi * 128))
                runs = []
                for p in pairs:
                    if runs and runs[-1][0][0] == p[0] and len(runs[-1]) < 4:
                        runs[-1].append(p)
                    else:
                        runs.append([p])
                for ri, run in enumerate(runs):
                    runs_all.append((qb, run, ri == 0, ri == len(runs) - 1))
            nr = len(runs_all)
            for i in range(nr + LA):
                if i % 3 == 2:
                    next(gen, None)
                if i < nr:
                    qb, run, isf, isl = runs_all[i]
                    qsl = slice(qb * 128, (qb + 1) * 128)
                    sb_ = 2 + (i % NSB)
                    PT = PTs[i % NPT]
                    ptk = ("PT", i % NPT)
                    n = len(run)
                    for j, (g, kb, mc) in enumerate(run):
                        sc.mm(bk(sb_)[:, j * 128:(j + 1) * 128], kT[:, kb * 128:(kb + 1) * 128], qk[b][g][:, qsl], True, True,
                              reads=[QK[3], QK[g]], writes=[PK[sb_]])
                    sc.add("act", lambda e, PT=PT, sb_=sb_, n=n: e.activation(out=PT[:, 0:n * 128], in_=bk(sb_)[:, 0:n * 128],
                                                                            func=AF.Exp, scale=scale),
                           reads=[PK[sb_]], writes=[ptk])
                    mc0 = run[0][2]
                    sc.add("dve",
                           lambda e, PT=PT, n=n, mc0=mc0: e.tensor_tensor(out=PT[:, 0:n * 128], in0=PT[:, 0:n * 128],
                                                                          in1=amask[:, mc0:mc0 + n * 128], op=ALU.mult),
                           reads=[ptk, "amask"], writes=[ptk])
                jx = i - LA
                if jx < 0:
                    continue
                qb, run, isf, isl = runs_all[jx]
                ob = qb % 2
                PT = PTs[jx % NPT]
                ptk = ("PT", jx % NPT)
                n = len(run)
                for j, (g, kb, mc) in enumerate(run):
                    sc.mm(bk(ob)[:, 0:129], PT[:, j * 128:(j + 1) * 128], V1[b][:, kb, :], isf and j == 0, isl and j == n - 1,
                          reads=[ptk, ("V1", b)], writes=[PK[ob]])
                if not isl:
                    continue
                yb = (qb // 4) % 2
                ytk = ("ytok", yb)
                sc.add("dve", lambda e, ob=ob, qb=qb: e.reciprocal(out=rden[:, qb:qb + 1], in_=bk(ob)[:, 128:129]),
                       reads=[PK[ob]], writes=[("rden", qb)])
                sc.add("act", lambda e, ob=ob, qb=qb, yb=yb: e.activation(
                    out=ytok[yb][:, (qb % 4) * 128:(qb % 4 + 1) * 128], in_=bk(ob)[:, 0:128], func=AF.Identity,
                    scale=rden[:, qb:qb + 1]), reads=[PK[ob], ("rden", qb)], writes=[ytk])
                if qb % 4 == 3:
                    pb = 7
                    for j in range(4):
                        sc.tr(bkb(pb)[:, j * 128:(j + 1) * 128], ytok[yb][:, j * 128:(j + 1) * 128], ident_b[:],
                              reads=[ytk, "identb"], writes=[PK[pb]])
                    tg = qb // 4
                    sc.add("dve", lambda e, pb=pb, tg=tg, hd=hd: e.tensor_copy(
                        out=ystg[hd % 2][:, tg * 512:(tg + 1) * 512], in_=bkb(pb)[:, 0:512]),
                        reads=[PK[pb]], writes=[("aystg", hd % 2)])
            for _ in gen:
                pass
            sc.dma("sp", yattnT[hd * 128:(hd + 1) * 128, :], ystg[hd % 2][:], reads=[("aystg", hd % 2)],
                   writes=[("yattnT", hd)])

        for _ in attn_prologue(0):
            pass
        for hd in range(8):
            attn_core(hd, attn_prologue(hd + 1) if hd + 1 < 8 else iter(()))
        sc.pop()
        if stop_after == f"attn{L}":
            break


        sc.push()
        sm = {}
        for nm in ("dtb", "alog"):
            sm[nm] = sc.tile([128, 32], F32, nm)
        sc.dma("sp", sm["dtb"][:], dtb[L], writes=["dtb"])
        sc.dma("sp", sm["alog"][:], alog[L], writes=["alog"])
        cw = sc.tile([128, 96], F32, "cw")
        cb = sc.tile([128, 24], F32, "cb")
        dsk = sc.tile([128, 16], F32, "dsk")
        snw = sc.tile([128, 16], F32, "snw")
        sc.dma("sp", cw[:], convw[L], writes=["cw"])
        sc.dma("sp", cb[:], convb[L], writes=["cb"])
        sc.dma("sp", dsk[:], dskip[L], writes=["dsk"])
        sc.dma("sp", snw[:], ssdn[L], writes=["snw"])
        dt_t = sc.tile([128, 512], F32, "dt_t")
        adt = sc.tile([128, 512], F32, "adt")
        ack = sc.tile([128, 512], F32, "ack")
        dtw = sc.tile([128, 512], F32, "dtw")
        cdec = sc.tile([128, 512], F32, "cdec")
        tmpA = sc.tile([128, 512], F32, "tmpA")
        apos = sc.tile([128, 32], F32, "apos")

        def v3(t):
            return t[:, :].rearrange("p (c h) -> p c h", h=32)

        def b3(t):
            return t[:, :].unsqueeze(1).broadcast_to([128, 16, 32])

        sc.add("dve", lambda e: e.tensor_tensor(out=v3(tmpA), in0=v3(dtraw), in1=b3(sm["dtb"]), op=ALU.add),
               reads=["dtraw", "dtb"], writes=["tmpA"])
        sc.add("act", lambda e: e.activation(out=tmpA[:], in_=tmpA[:], func=AF.Exp), reads=["tmpA"], writes=["tmpA"])
        sc.add("act", lambda e: e.activation(out=dt_t[:], in_=tmpA[:], func=AF.Ln, bias=1.0, scale=1.0),
               reads=["tmpA"], writes=["dt_t"])
        sc.add("act", lambda e: e.activation(out=apos[:], in_=sm["alog"][:], func=AF.Exp), reads=["alog"], writes=["apos"])
        sc.add("dve", lambda e: e.scalar_tensor_tensor(out=v3(adt), in0=v3(dt_t), scalar=-1.0, in1=b3(apos),
                                                     op0=ALU.mult, op1=ALU.mult),
               reads=["dt_t", "apos"], writes=["adt"])
        adt3 = [sc.tile([128, 512], BF16, "adt3") for _ in range(3)]
        rres = sc.tile([128, 512], F32, "rres")
        sc.add("dve", lambda e: e.tensor_copy(out=adt3[0][:], in_=adt[:]), reads=["adt"], writes=[("adt3", 0)])
        sc.add("dve", lambda e: e.tensor_tensor(out=rres[:], in0=adt[:], in1=adt3[0][:], op=ALU.subtract),
               reads=["adt", ("adt3", 0)], writes=["rres"])
        sc.add("dve", lambda e: e.tensor_copy(out=adt3[1][:], in_=rres[:]), reads=["rres"], writes=[("adt3", 1)])
        sc.add("dve", lambda e: e.tensor_tensor(out=rres[:], in0=rres[:], in1=adt3[1][:], op=ALU.subtract),
               reads=["rres", ("adt3", 1)], writes=["rres"])
        sc.add("dve", lambda e: e.tensor_copy(out=adt3[2][:], in_=rres[:]), reads=["rres"], writes=[("adt3", 2)])
        sc.mm(bk(7)[:, :], tri_f[:], adt[:], True, True, reads=["trif", "adt"], writes=[PK[7]])
        sc.mm(bk(1)[:, :], ones_f[:], adt[:], True, True, reads=["onesf", "adt"], writes=[PK[1]])
        sc.add("dve", lambda e: e.tensor_copy(out=ack[:], in_=bk(7)[:, :]), reads=[PK[7]], writes=["ack"])
        nack = sc.tile([128, 512], F32, "nack")
        sc.add("dve", lambda e: e.tensor_scalar(out=nack[:], in0=ack[:], scalar1=-1.0, scalar2=None, op0=ALU.mult),
               reads=["ack"], writes=["nack"])
        sc.add("dve", lambda e: e.tensor_tensor(out=tmpA[:], in0=bk(1)[:, :], in1=ack[:], op=ALU.subtract),
               reads=[PK[1], "ack", "tmpA"], writes=["tmpA"])
        sc.add("act", lambda e: e.activation(out=tmpA[:], in_=tmpA[:], func=AF.Exp), reads=["tmpA"], writes=["tmpA"])
        sc.add("dve", lambda e: e.tensor_tensor(out=dtw[:], in0=dt_t[:], in1=tmpA[:], op=ALU.mult),
               reads=["tmpA", "dt_t"], writes=["dtw"])
        sc.add("act", lambda e: e.activation(out=cdec[:], in_=bk(1)[:, :], func=AF.Exp), reads=[PK[1]], writes=["cdec"])

        pre = [sc.tile([128, 3 + S], BF16, "pre") for _ in range(2)]
        for i in range(2):
            sc.add("pool", lambda e, i=i: e.memset(pre[i][:, 0:3], 0.0), writes=[("pre", i)])
        xc = [sc.tile([128, S], BF16, "xc") for _ in range(6)]
        diag = [sc.tile([128, 128], BF16, "diag") for _ in range(8)]
        sz = [sc.tile([128, 4, 512], BF16, "sz") for _ in range(2)]
        xdt = [sc.tile([128, 512], BF16, "xdt") for _ in range(2)]
        xdw = [sc.tile([128, 512], BF16, "xdw") for _ in range(2)]
        btok = [sc.tile([128, 128], BF16, "btok") for _ in range(2)]
        cbm = [sc.tile([128, 128], F32, "cbm") for _ in range(2)]
        HT = sc.tile([128, 512], F32, "HT")
        HTb = [sc.tile([128, 512], BF16, "HTb") for _ in range(2)]
        argt = [sc.tile([128, 128], F32, "arg") for _ in range(16)]
        dect = [sc.tile([128, 128], F32, "dec") for _ in range(16)]
        Mh = [sc.tile([128, 128], BF16, "Mh") for _ in range(16)]
        Et = [sc.tile([128, 128], F32, "E") for _ in range(16)]
        Ct = [sc.tile([128, 128], BF16, "Ct") for _ in range(16)]
        yv = [sc.tile([128, 128], F32, "yv") for _ in range(4)]
        yg = [sc.tile([128, 128], F32, "yg") for _ in range(8)]
        sq = [sc.tile([128, 128], BF16, "sq") for _ in range(4)]
        rbc = sc.tile([128, 128], F32, "rbc")
        yst = [sc.tile([128, 4, 512], BF16, "yst") for _ in range(2)]
        yssd_v = yssdT.rearrange("(k p) t -> p k t", p=128)
        projT_v = projT.rearrange("(k p) t -> p k t", p=128)
        YK = [PK[5]] * 4
        BANKK = {6: [PK[6]], 7: [PK[7]]}
        nd = 0
        for gi in range(4):
            chunks = [4 * gi + i for i in range(4)] + [16 + gi, 20 + gi]
            for j, chx in enumerate(chunks):
                dg = []
                for k in range(4):
                    di = nd % 8
                    nd += 1
                    col = chx * 4 + k
                    sc.add("act", lambda e, di=di, col=col: e.activation(
                        out=diag[di][:], in_=ident_f[:], func=AF.Identity, scale=cw[:, col:col + 1]),
                        reads=["identf", "cw"], writes=[("diag", di)])
                    dg.append(di)
                pi = (gi * 6 + j) % 2
                r0 = R_XBC + chx * 128
                sc.dma("sp", pre[pi][:, 3:3 + S], projT[r0:r0 + 128, :], reads=PJ(r0), writes=[("pre", pi)])
                for tg in range(TG):
                    pb = 6 + (tg % 2)
                    for k in range(4):
                        sc.mm(bk(pb)[:, :], diag[dg[k]][:], pre[pi][:, tg * 512 + k:tg * 512 + k + 512], k == 0, k == 3,
                              reads=[("diag", dg[k]), ("pre", pi)], writes=BANKK[pb])
                    sc.add("act", lambda e, j=j, tg=tg, pb=pb, chx=chx: e.activation(
                        out=xc[j][:, tg * 512:(tg + 1) * 512], in_=bk(pb)[:, :], func=AF.Silu, bias=cb[:, chx:chx + 1]),
                        reads=BANKK[pb] + ["cb"], writes=[("xc", j)])
            BT, CT = xc[4], xc[5]

            def bcreg(s, eh):
                return bk(3 + eh // 4)[:, (eh % 4) * 128:(eh % 4 + 1) * 128], PK[3 + eh // 4]

            def stage0(c):
                if "nostage0" in DBG:
                    return
                s = c % 2
                tsl = slice(c * 128, (c + 1) * 128)
                hsl = slice(c * 32 + gi * 8, c * 32 + gi * 8 + 8)
                for jx in range(4):
                    sc.tr(bkb(0)[:, jx * 128:(jx + 1) * 128], xc[jx][:, tsl], ident_b[:],
                          reads=[("xc", jx), "identb"], writes=[PK[0]])
                sc.add("dve", lambda e: e.tensor_tensor(
                    out=xdt[s][:, :].rearrange("p (h d) -> p h d", h=8),
                    in0=bkb(0)[:, 0:512].rearrange("p (h d) -> p h d", h=8),
                    in1=dt_t[:, hsl].unsqueeze(2).broadcast_to([128, 8, 64]), op=ALU.mult),
                    reads=[PK[0], "dt_t"], writes=[("xdt", s)])
                sc.add("dve", lambda e: e.tensor_tensor(
                    out=xdw[s][:, :].rearrange("p (h d) -> p h d", h=8),
                    in0=bkb(0)[:, 0:512].rearrange("p (h d) -> p h d", h=8),
                    in1=dtw[:, hsl].unsqueeze(2).broadcast_to([128, 8, 64]), op=ALU.mult),
                    reads=[PK[0], "dtw"], writes=[("xdw", s)])
                sc.tr(bkb(2)[:, 0:128], BT[:, tsl], ident_b[:], reads=[("xc", 4), "identb"], writes=[PK[2]])
                sc.add("dve", lambda e: e.tensor_copy(out=btok[s][:], in_=bkb(2)[:, 0:128]), reads=[PK[2]], writes=[("btok", s)])
                sc.mm(bk(2)[:, 128:256], BT[:, tsl], CT[:, tsl], True, True, reads=[("xc", 4), ("xc", 5)], writes=[PK[2]])
                sc.add("dve", lambda e: e.tensor_tensor(out=cbm[s][:], in0=bk(2)[:, 128:256], in1=tri_f[:], op=ALU.mult),
                       reads=[PK[2], "trif"], writes=[("cbm", s)])

            def bc_mms(c):
                s = c % 2
                for eh in range(8):
                    hcol = c * 32 + gi * 8 + eh
                    reg, rk = bcreg(s, eh)
                    for t3 in range(3):
                        sc.mm(reg, adt3[t3][:, hcol:hcol + 1].broadcast_to([128, 128]), tri_b[:], t3 == 0, t3 == 2,
                              reads=[("adt3", t3), "trib"], writes=[rk])

            def st_mm(c):
                s = c % 2
                if c < NT - 1:
                    sc.mm(bk(1)[:, :], btok[s][:], xdw[s][:], True, True, reads=[("btok", s), ("xdw", s)], writes=[PK[1]])

            def heads(c, part):
                s = c % 2
                tsl = slice(c * 128, (c + 1) * 128)
                if part == 0:
                    for eh in range(1, 8, 2):
                        hcol = c * 32 + gi * 8 + eh
                        reg, rk = bcreg(s, eh)
                        bi = s * 8 + eh
                        sc.add("act", lambda e, reg=reg, bi=bi, hcol=hcol: e.activation(
                            out=argt[bi][:], in_=reg, func=AF.Relu, bias=ack[:, hcol:hcol + 1], scale=-1.0),
                            reads=[rk, "ack"], writes=[("arg", bi)])
                    for eh in range(8):
                        hcol = c * 32 + gi * 8 + eh
                        reg, rk = bcreg(s, eh)
                        bi = s * 8 + eh
                        if eh % 2 == 1:
                            sc.add("act", lambda e, bi=bi: e.activation(out=dect[bi][:], in_=argt[bi][:], func=AF.Exp,
                                                                        scale=-1.0),
                                   reads=[("arg", bi)], writes=[("dec", bi)])
                        else:
                            sc.add("act", lambda e, reg=reg, bi=bi, hcol=hcol: e.activation(
                                out=dect[bi][:], in_=reg, func=AF.Exp, bias=nack[:, hcol:hcol + 1], scale=1.0),
                                reads=[rk, "nack"], writes=[("dec", bi)])
                    if c > 0:
                        for eh in range(8):
                            reg, rk = bcreg(s, eh)
                            bi = s * 8 + eh
                            sc.add("act", lambda e, reg=reg, bi=bi: e.activation(out=Et[bi][:], in_=reg, func=AF.Exp),
                                   reads=[rk], writes=[("E", bi)])
                    return
                if part == 1:
                    for eh in range(8):
                        bi = s * 8 + eh
                        if eh % 2 == 0:
                            sc.add("dve", lambda e, bi=bi: e.scalar_tensor_tensor(
                                out=Mh[bi][:], in0=dect[bi][:], scalar=1.0, in1=cbm[s][:], op0=ALU.min, op1=ALU.mult),
                                reads=[("cbm", s), ("dec", bi)], writes=[("Mh", bi)])
                        else:
                            sc.add("pool", lambda e, bi=bi: e.tensor_tensor(out=Mh[bi][:], in0=cbm[s][:], in1=dect[bi][:],
                                                                          op=ALU.mult),
                                   reads=[("cbm", s), ("dec", bi)], writes=[("Mh", bi)])
                    return
                if part == 2:
                    if c > 0:
                        for eh in range(8):
                            bi = s * 8 + eh
                            sc.add("pool", lambda e, bi=bi: e.tensor_tensor(out=Ct[bi][:], in0=CT[:, tsl], in1=Et[bi][:],
                                                                          op=ALU.mult),
                                   reads=[("xc", 5), ("E", bi)], writes=[("Ct", bi)])
                    return
                for eh in range(8):
                    bi = s * 8 + eh
                    pr, e2 = eh // 2, eh % 2
                    yreg = bk(5)[e2 * 64:(e2 + 1) * 64, pr * 128:(pr + 1) * 128]
                    sc.mm(yreg, xdt[s][:, eh * 64:(eh + 1) * 64], Mh[bi][:], True, c == 0,
                          reads=[("xdt", s), ("Mh", bi)], writes=[YK[pr]])
                    if c > 0:
                        sc.mm(yreg, HTb[(c - 1) % 2][:, eh * 64:(eh + 1) * 64], Ct[bi][:], False, True,
                              reads=[("HTb", (c - 1) % 2), ("Ct", bi)], writes=[YK[pr]])

            def state_upd(c):
                hsl = slice(c * 32 + gi * 8, c * 32 + gi * 8 + 8)
                if c < NT - 1:
                    if c == 0:
                        sc.add("dve", lambda e: e.tensor_copy(out=HT[:], in_=bk(1)[:, :]), reads=[PK[1]], writes=["HT"])
                    else:
                        sc.add("dve", lambda e: e.tensor_tensor(
                            out=HT[:, :].rearrange("p (h d) -> p h d", h=8),
                            in0=HT[:, :].rearrange("p (h d) -> p h d", h=8),
                            in1=cdec[:, hsl].unsqueeze(2).broadcast_to([128, 8, 64]), op=ALU.mult),
                            reads=["HT", "cdec"], writes=["HT"])
                        sc.add("dve", lambda e: e.tensor_tensor(out=HT[:], in0=HT[:], in1=bk(1)[:, :], op=ALU.add),
                               reads=["HT", PK[1]], writes=["HT"])
                    sc.add("act", lambda e: e.copy(out=HTb[c % 2][:], in_=HT[:]), reads=["HT"], writes=[("HTb", c % 2)])

            def tail_a(c):
                g2 = (c % 2) * 4
                tg = c // 4
                zi = (gi * TG + tg) % 2
                cc = c % 4
                tsl = slice(c * 128, (c + 1) * 128)
                hsl = slice(c * 32 + gi * 8, c * 32 + gi * 8 + 8)
                for pr in range(4):
                    dcol = gi * 4 + pr
                    sc.add("dve", lambda e, pr=pr, dcol=dcol: e.scalar_tensor_tensor(
                        out=yv[pr][:], in0=xc[pr][:, tsl], scalar=dsk[:, dcol:dcol + 1], in1=bk(5)[:, pr * 128:(pr + 1) * 128],
                        op0=ALU.mult, op1=ALU.add), reads=[("xc", pr), "dsk", YK[pr]], writes=[("yv", pr)])
                    sc.add("dve", lambda e, pr=pr, g2=g2: e.tensor_tensor(
                        out=yg[g2 + pr][:], in0=yv[pr][:], in1=sz[zi][:, pr, cc * 128:(cc + 1) * 128], op=ALU.mult),
                        reads=[("yv", pr), ("sz", zi)], writes=[("yg", g2 + pr)])
                    sc.add("act", lambda e, pr=pr, g2=g2: e.activation(out=sq[pr][:], in_=yg[g2 + pr][:], func=AF.Square),
                           reads=[("yg", g2 + pr)], writes=[("sq", pr)])

            def tail_a_pe(c):
                for pr in range(4):
                    sc.mm(bk(6)[:, 0:128], ones_b[:], sq[pr][:], pr == 0, pr == 3,
                          reads=["onesb", ("sq", pr)], writes=[PK[6]])

            def tail_a2(c):
                sc.add("act", lambda e: e.activation(out=rbc[:], in_=bk(6)[:, 0:128], func=AF.Ln, bias=epsb[:, 0:1],
                                                   scale=1.0 / 512.0), reads=[PK[6], "epsb"], writes=["rbc"])
                sc.add("act", lambda e: e.activation(out=rbc[:], in_=rbc[:], func=AF.Exp, scale=-0.5),
                       reads=["rbc"], writes=["rbc"])

            def tail_b(c):
                g2 = (c % 2) * 4
                tg = c // 4
                ysi = (gi * TG + tg) % 2
                cc = c % 4
                for pr in range(4):
                    dcol = gi * 4 + pr
                    sc.add("dve", lambda e, pr=pr, dcol=dcol: e.scalar_tensor_tensor(
                        out=yst[ysi][:, pr, cc * 128:(cc + 1) * 128], in0=yg[g2 + pr][:], scalar=snw[:, dcol:dcol + 1],
                        in1=rbc[:], op0=ALU.mult, op1=ALU.mult),
                        reads=[("yg", g2 + pr), "snw", "rbc"], writes=[("yst", ysi)])
                if c % 4 == 3:
                    sc.dma("sp", yssd_v[:, gi * 4:(gi + 1) * 4, tg * 512:(tg + 1) * 512], yst[ysi][:],
                           reads=[("yst", ysi)], writes=[("yssdT", gi * 4 + k) for k in range(4)])

            stage0(0)
            bc_mms(0)
            heads(0, 0)
            heads(0, 1)
            bc_mms(1)
            heads(1, 0)
            for c in range(NT):
                tg = c // 4
                zi = (gi * TG + tg) % 2
                if c % 4 == 0:
                    sc.dma("sp", sz[zi][:], projT_v[:, gi * 4:(gi + 1) * 4, tg * 512:(tg + 1) * 512],
                           reads=PJ(R_Z + gi * 512, 512), writes=[("sz", zi)])
                st_mm(c)
                state_upd(c)
                if c + 1 < NT:
                    heads(c + 1, 2)
                heads(c, 3)
                tail_a(c)
                if c + 1 < NT:
                    stage0(c + 1)
                if c + 2 < NT:
                    bc_mms(c + 2)
                    heads(c + 2, 0)
                tail_a_pe(c)
                if c + 1 < NT:
                    heads(c + 1, 1)
                if c > 0:
                    tail_b(c - 1)
                tail_a2(c)
            tail_b(NT - 1)
        sc.pop()
        if stop_after == f"ssd{L}":
            break

        sc.push()
        ys = sc.tile([128, 16, S], BF16, "ys")
        ya = sc.tile([128, 8, S], BF16, "ya")
        mg = sc.tile([128, 8, S], BF16, "mg")
        wo = sc.tile([128, 8, D], BF16, "wo")
        for k in range(16):
            sc.dma("sp", ys[:, k, :], yssdT[k * 128:(k + 1) * 128, :], reads=[("yssdT", k)], writes=[("ys", k)])
        for k in range(8):
            sc.dma("sp", ya[:, k, :], yattnT[k * 128:(k + 1) * 128, :], reads=[("yattnT", k)], writes=[("ya", k)])
        for c in range(8):
            load_w("pool", wo[:, c, :], w_o[L, c * 128:(c + 1) * 128, :], ("wo", c))
        sc.push()
        was = [sc.tile([128, 16, 128], BF16, "wa") for _ in range(2)]
        wbs = [sc.tile([128, 8, 128], BF16, "wb") for _ in range(2)]
        gss = [sc.tile([128, S], BF16, "gs") for _ in range(2)]
        gas = [sc.tile([128, S], BF16, "ga") for _ in range(2)]
        m1 = [sc.tile([128, 512], F32, "m1") for _ in range(2)]
        m2 = [sc.tile([128, 512], F32, "m2") for _ in range(2)]
        w_a_v = w_a[L].rearrange("(k p) n -> p k n", p=128)
        w_b_v = w_b[L].rearrange("(k p) n -> p k n", p=128)
        YSK = [("ys", k) for k in range(16)]
        YAK = [("ya", k) for k in range(8)]
        for oc in range(8):
            i2 = oc % 2
            load_w("pool", was[i2][:], w_a_v[:, :, oc * 128:(oc + 1) * 128], ("wa", i2))
            load_w("pool", wbs[i2][:], w_b_v[:, :, oc * 128:(oc + 1) * 128], ("wb", i2))
            r0 = R_GS + oc * 128
            sc.dma("sp", gss[i2][:], projT[r0:r0 + 128, :], reads=PJ(r0), writes=[("gs", i2)])
            r0 = R_GA + oc * 128
            sc.dma("sp", gas[i2][:], projT[r0:r0 + 128, :], reads=PJ(r0), writes=[("ga", i2)])
            for tg in range(TG):
                sl = slice(tg * 512, (tg + 1) * 512)
                pa, pb = 0 + (tg % 2), 2 + (tg % 2)
                for k in range(16):
                    sc.mm(bk(pa)[:, :], was[i2][:, k, :], ys[:, k, sl], k == 0, k == 15, reads=[("wa", i2), YSK[k]],
                          writes=[PK[pa]])
                for k in range(8):
                    sc.mm(bk(pb)[:, :], wbs[i2][:, k, :], ya[:, k, sl], k == 0, k == 7, reads=[("wb", i2), YAK[k]],
                          writes=[PK[pb]])
                t = tg % 2
                sc.add("dve", lambda e, t=t, pa=pa, i2=i2, sl=sl: e.tensor_tensor(out=m1[t][:], in0=bk(pa)[:, :],
                                                                                in1=gss[i2][:, sl], op=ALU.mult),
                       reads=[PK[pa], ("gs", i2)], writes=[("m1", t)])
                sc.add("dve", lambda e, t=t, pb=pb, i2=i2, sl=sl: e.tensor_tensor(out=m2[t][:], in0=bk(pb)[:, :],
                                                                                in1=gas[i2][:, sl], op=ALU.mult),
                       reads=[PK[pb], ("ga", i2)], writes=[("m2", t)])
                sc.add("dve", lambda e, t=t, oc=oc, sl=sl: e.tensor_tensor(out=mg[:, oc, sl], in0=m1[t][:], in1=m2[t][:],
                                                                          op=ALU.add),
                       reads=[("m1", t), ("m2", t)], writes=[("mg", oc)])
        sc.pop()
        hts = [sc.tile([128, D], F32, "hres") for _ in range(2)]
        MGK = [("mg", k) for k in range(8)]
        WOK = [("wo", k) for k in range(8)]
        for tb in range(NT):
            hb = tb % 2
            hk = ("hres", hb)
            sc.dma("sp", hts[hb][:], hD[tb * 128:(tb + 1) * 128, :], reads=[("h", tb)], writes=[hk])
            for half in range(2):
                pb = 4 + half
                for c in range(8):
                    sc.mm(bk(pb)[:, :], mg[:, c, tb * 128:(tb + 1) * 128], wo[:, c, half * 512:(half + 1) * 512], c == 0,
                          c == 7, reads=[MGK[c], WOK[c]], writes=[PK[pb]])
                sc.add("dve", lambda e, hb=hb, half=half, pb=pb: e.tensor_tensor(
                    out=hts[hb][:, half * 512:(half + 1) * 512], in0=hts[hb][:, half * 512:(half + 1) * 512],
                    in1=bk(pb)[:, :], op=ALU.add), reads=[hk, PK[pb]], writes=[hk])
            sc.dma("sp", hD[tb * 128:(tb + 1) * 128, :], hts[hb][:], reads=[hk], writes=[("h", tb)])
        sc.pop()
        if stop_after == f"merge{L}":
            break

        sc.push()
        uT = sc.tile([128, 8, S], BF16, "uT2")
        norm_to_uT(uT, nffn[L], f"n{L}b")
        wd = sc.tile([128, NJ, D], BF16, "wd")
        for j in range(NJ):
            load_w("pool", wd[:, j, :], w_d[L, j * 128:(j + 1) * 128, :], ("wd", j))
        WDK = [("wd", j) for j in range(NJ)]
        actT = sc.tile([128, NJ, 1024], BF16, "actT")
        wgs = [sc.tile([128, 8, 128], BF16, "wg") for _ in range(2)]
        wus = [sc.tile([128, 8, 128], BF16, "wu") for _ in range(2)]
        sgt = [sc.tile([128, 512], F32, "sg") for _ in range(2)]
        hts = [sc.tile([128, D], F32, "hres2") for _ in range(2)]
        w_gu_v = w_gu[L].rearrange("(c p) n -> p c n", p=128)
        AK = [("actT", j) for j in range(NJ)]
        nw_ = 0
        for th in range(2):
            for j in range(NJ):
                i2 = nw_ % 2
                nw_ += 1
                load_w("pool", wgs[i2][:], w_gu_v[:, :, j * 128:(j + 1) * 128], ("wg", i2))
                load_w("pool", wus[i2][:], w_gu_v[:, :, FFN + j * 128:FFN + (j + 1) * 128], ("wu", i2))
                for t2 in range(2):
                    tg = th * 2 + t2
                    sl = slice(tg * 512, (tg + 1) * 512)
                    pg, pu = 0 + t2, 2 + t2
                    for c in range(8):
                        sc.mm(bk(pg)[:, :], wgs[i2][:, c, :], uT[:, c, sl], c == 0, c == 7, reads=[("wg", i2), UK[tg]],
                              writes=[PK[pg]])
                    for c in range(8):
                        sc.mm(bk(pu)[:, :], wus[i2][:, c, :], uT[:, c, sl], c == 0, c == 7, reads=[("wu", i2), UK[tg]],
                              writes=[PK[pu]])
                    sc.add("act", lambda e, t2=t2, pg=pg: e.activation(out=sgt[t2][:], in_=bk(pg)[:, :], func=AF.Silu),
                           reads=[PK[pg]], writes=[("sg", t2)])
                    sc.add("dve", lambda e, t2=t2, pu=pu, j=j: e.tensor_tensor(
                        out=actT[:, j, t2 * 512:(t2 + 1) * 512], in0=sgt[t2][:], in1=bk(pu)[:, :], op=ALU.mult),
                        reads=[("sg", t2), PK[pu]], writes=[AK[j]])
            for t8 in range(8):
                tb = th * 8 + t8
                hb = tb % 2
                hk = ("hres2", hb)
                sc.dma("sp", hts[hb][:], hD[tb * 128:(tb + 1) * 128, :], reads=[("h", tb)], writes=[hk])
                for half in range(2):
                    pb = 4 + half
                    for j in range(NJ):
                        sc.mm(bk(pb)[:, :], actT[:, j, t8 * 128:(t8 + 1) * 128], wd[:, j, half * 512:(half + 1) * 512],
                              j == 0, j == NJ - 1, reads=[AK[j], WDK[j]], writes=[PK[pb]])
                    sc.add("dve", lambda e, hb=hb, half=half, pb=pb: e.tensor_tensor(
                        out=hts[hb][:, half * 512:(half + 1) * 512], in0=hts[hb][:, half * 512:(half + 1) * 512],
                        in1=bk(pb)[:, :], op=ALU.add), reads=[hk, PK[pb]], writes=[hk])
                sc.dma("sp", hD[tb * 128:(tb + 1) * 128, :], hts[hb][:], reads=[hk], writes=[("h", tb)])
        sc.pop()
        if stop_after == f"ffn{L}":
            break

    sc.sb_stack = []
    sc.push()
    nf = sc.tile([128, D], F32, "nf")
    sc.dma("sp", nf[:], nfin[:, :], writes=["nf"])
    hall = sc.tile([128, NT, D], F32, "hallf")
    ots = [sc.tile([128, D], F32, "ofin") for _ in range(3)]
    junk = sc.tile([128, D], BF16, "junkf")
    st = sc.tile([128, 2 * NT], F32, "fstat")
    OUTK = []
    for tb in range(NT):
        sc.dma("sp", hall[:, tb, :], hD[tb * 128:(tb + 1) * 128, :], reads=[("h", tb)], writes=[("hallf", tb)])
        sc.add("act", lambda e, tb=tb: e.activation(out=junk[:], in_=hall[:, tb, :], func=AF.Square,
                                                  accum_out=st[:, tb:tb + 1]),
               reads=[("hallf", tb)], writes=["junkf", ("fss", tb)])
    sc.add("act", lambda e: e.activation(out=st[:, NT:2 * NT], in_=st[:, 0:NT], func=AF.Ln, bias=epsb[:, 0:1],
                                       scale=1.0 / D), reads=[("fss", tb) for tb in range(NT)] + ["epsb"], writes=["frstd"])
    sc.add("act", lambda e: e.activation(out=st[:, NT:2 * NT], in_=st[:, NT:2 * NT], func=AF.Exp, scale=-0.5),
           reads=["frstd"], writes=["frstd"])
    for tb in range(NT):
        b = tb % 3
        sc.add("act", lambda e, b=b, tb=tb: e.activation(out=ots[b][:], in_=hall[:, tb, :], func=AF.Identity,
                                                       scale=st[:, NT + tb:NT + tb + 1]),
               reads=[("hallf", tb), "frstd"], writes=[("ofin", b)])
        sc.add("dve", lambda e, b=b: e.tensor_tensor(out=ots[b][:], in0=ots[b][:], in1=nf[:], op=ALU.mult),
               reads=[("ofin", b), "nf"], writes=[("ofin", b)])
        sc.dma("sp", out[tb * 128:(tb + 1) * 128, :], ots[b][:], reads=[("ofin", b)], writes=[("out", tb)])
        OUTK.append(("out", tb))
    allk = list(OUTK) + [k for k in sc.lastw.keys() if isinstance(k, tuple) and k[0] in ("projT", "yssdT", "yattnT", "h")]
    sc.add("sp", lambda e: e.nop(), reads=allk)
    sc.pop()

    sems = {}
    for e in Sched.ENGS:
        sems[e] = nc.alloc_semaphore(f"sem_{e}")
    dsems = [nc.alloc_semaphore(f"dsem{i}") for i in range(NDSEM)]
    stats = sc.emit(sems, dsems)
    return stats


def host_constants():
    a = np.arange(128)
    dab = a[None, :] - a[:, None]
    ge = (dab >= 0)
    le = (dab <= 0)

    def m(cond):
        return cond.astype(np.float32)
    g0 = np.concatenate([m(le), m(ge)], axis=1)
    mod4 = (dab % 4 == 0)
    g1 = np.concatenate([m(le & mod4)] + [m(mod4)] * 3 + [m(ge & mod4)], axis=1)
    mod16 = (dab % 16 == 0)
    g2 = np.concatenate([m(mod16)] * 15 + [m(ge & mod16)], axis=1)
    amask = np.concatenate([g0, g1, g2], axis=1)
    assert amask.shape == (128, 2944)
    tri = (a[:, None] <= a[None, :]).astype(np.float32)
    ident = np.eye(128, dtype=np.float32)
    half = 16
    inv = 500000.0 ** (-np.arange(0, 32, 2, dtype=np.float64) / 32.0)
    ang = np.arange(S, dtype=np.float64)[None, :] * inv[:, None]
    cos, sin = np.cos(ang), np.sin(ang)
    C32 = np.concatenate([cos, cos], axis=0)
    S32 = np.concatenate([-sin, sin], axis=0)
    rope = np.concatenate([C32, S32], axis=1).astype(np.float32)
    rsw = np.zeros((32, 32), np.float32)
    for mm_ in range(32):
        rsw[(mm_ + half) % 32, mm_] = 1.0
    return dict(c_amask=amask, c_tri=tri, c_ident=ident, c_rope=rope, c_rsw=rsw)


def host_layout(inputs):
    f = lambda a: np.ascontiguousarray(np.asarray(a, dtype=np.float32))
    m = {}
    for n in ("w_in", "w_ssd_branch", "w_attn_branch", "w_out", "w_gate_up", "w_down"):
        m[n] = f(inputs[n])
    m["nmix"] = f(np.asarray(inputs["norm_mix"]).reshape(DEPTH, 8, 128).transpose(0, 2, 1))
    m["nffn"] = f(np.asarray(inputs["norm_ffn"]).reshape(DEPTH, 8, 128).transpose(0, 2, 1))
    m["nfin"] = f(np.broadcast_to(np.asarray(inputs["norm_final"])[None, :], (128, D)))
    cw = np.asarray(inputs["conv_w"])
    m["convw"] = f(cw.transpose(0, 2, 1).reshape(DEPTH, 24, 128, 4).transpose(0, 2, 1, 3).reshape(DEPTH, 128, 96))
    m["convb"] = f(np.asarray(inputs["conv_b"]).reshape(DEPTH, 24, 128).transpose(0, 2, 1))
    m["dtb"] = f(np.broadcast_to(np.asarray(inputs["dt_bias"])[:, None, :], (DEPTH, 128, 32)))
    m["alog"] = f(np.broadcast_to(np.asarray(inputs["a_log"])[:, None, :], (DEPTH, 128, 32)))
    m["dskip"] = f(np.repeat(np.asarray(inputs["d_skip"]), 64, axis=1).reshape(DEPTH, 16, 128).transpose(0, 2, 1))
    m["ssdn"] = f(np.asarray(inputs["ssd_norm"]).reshape(DEPTH, 16, 128).transpose(0, 2, 1))
    m.update(host_constants())
    return m


_CACHE = {}


def kernel(**inputs):
    x = np.asarray(inputs["x"], dtype=np.float32)
    B = x.shape[0]
    shared = host_layout(inputs)
    if "nc" not in _CACHE:
        nc = bass.Bass("TRN2", target_bir_lowering=False)
        build(nc)
        _CACHE["nc"] = nc
    nc = _CACHE["nc"]
    in_maps = []
    for b in range(B):
        d = dict(shared)
        d["x"] = np.ascontiguousarray(x[b])
        in_maps.append(d)
    res = run_bass_kernel_spmd(nc, in_maps, core_ids=list(range(B)))
    return np.stack([np.asarray(r["out"], dtype=np.float32) for r in res.results], axis=0)
```
